# Optimizing a Trainium2 kernel written in Bass

```python
import math
import jax, jax.numpy as jnp
from jax import lax
import numpy as np

D_MODEL = 1024
BATCH = 16
SEQ = 2048
DEPTH = 2

D_MIX = D_MODEL
D_GMLP = D_MIX // 2
D_MLA = D_MIX - D_GMLP
GMLP_GROUPS = 8
GMLP_GROUP_DIM = D_GMLP // GMLP_GROUPS
CHUNK = 128
MLA_HEADS = 8
QK_NOPE_DIM = 64
QK_ROPE_DIM = 32
V_HEAD_DIM = D_MLA // MLA_HEADS
Q_RANK = D_MODEL // 4
KV_RANK = D_MODEL // 8
ROPE_THETA = 10000.0
Q_BLOCK = 128
D_FF = 4 * D_MODEL
N_MOD = 6
EPS = 1e-6
D_IN = 2 * D_GMLP + Q_RANK + KV_RANK + QK_ROPE_DIM

kernel_name = "hybrid_gmlp_mla_adaln_block"


def rmsnorm(x, g):
    xf = x.astype(jnp.float32)
    y = xf * lax.rsqrt(jnp.mean(xf * xf, axis=-1, keepdims=True) + EPS)
    return (y * g.astype(jnp.float32)).astype(x.dtype)


def layernorm_noaffine(x):
    xf = x.astype(jnp.float32)
    mu = jnp.mean(xf, axis=-1, keepdims=True)
    d = xf - mu
    y = d * lax.rsqrt(jnp.mean(d * d, axis=-1, keepdims=True) + EPS)
    return y.astype(x.dtype)


def rope_tables(positions, dim):
    freqs = ROPE_THETA ** (-jnp.arange(0, dim, 2, dtype=jnp.float32) / dim)
    ang = positions.astype(jnp.float32)[..., None] * freqs
    return jnp.cos(ang), jnp.sin(ang)


def apply_rope(x, cos, sin):
    half = x.shape[-1] // 2
    x1, x2 = x[..., :half], x[..., half:]
    cos = cos.astype(x.dtype)
    sin = sin.astype(x.dtype)
    return jnp.concatenate([x1 * cos - x2 * sin, x1 * sin + x2 * cos], axis=-1)


def gmlp_mixer(u, v, w_s, b_s):
    B, S, G, Dg = u.shape
    n_chunks = S // CHUNK
    u = jax.nn.gelu(u)
    v = layernorm_noaffine(jax.nn.gelu(v))
    causal = jnp.tril(jnp.ones((CHUNK, CHUNK), dtype=bool))
    w = jnp.where(causal[None], w_s, 0.0)
    vc = v.reshape(B, n_chunks, CHUNK, G, Dg)
    mixed = jnp.einsum('gts,bcsgd->bctgd', w, vc) + b_s.T[None, None, :, :, None]
    return u * mixed.reshape(B, S, G, Dg)


def mla_mixer(q_lat, kv_lat, k_rope_raw, cos, sin, g_q, g_kv, w_uq, w_ukv):
    B, S, _ = q_lat.shape
    c_q = rmsnorm(q_lat, g_q)
    q = (c_q @ w_uq).reshape(B, S, MLA_HEADS, QK_NOPE_DIM + QK_ROPE_DIM)
    q_nope, q_rope = q[..., :QK_NOPE_DIM], q[..., QK_NOPE_DIM:]
    q_rope = apply_rope(q_rope, cos[:, :, None, :], sin[:, :, None, :])
    c_kv = rmsnorm(kv_lat, g_kv)
    kv = (c_kv @ w_ukv).reshape(B, S, MLA_HEADS, QK_NOPE_DIM + V_HEAD_DIM)
    k_nope, v = kv[..., :QK_NOPE_DIM], kv[..., QK_NOPE_DIM:]
    k_rope = apply_rope(k_rope_raw, cos, sin)
    scale = (QK_NOPE_DIM + QK_ROPE_DIM) ** -0.5
    outs = []
    for i in range(S // Q_BLOCK):
        q0 = i * Q_BLOCK
        kend = q0 + Q_BLOCK
        s = (jnp.einsum('bqhd,bkhd->bhqk', q_nope[:, q0:kend], k_nope[:, :kend])
             + jnp.einsum('bqhr,bkr->bhqk', q_rope[:, q0:kend], k_rope[:, :kend]))
        s = s.astype(jnp.float32) * scale
        qpos = q0 + jnp.arange(Q_BLOCK)
        kpos = jnp.arange(kend)
        s = jnp.where(kpos[None, :] <= qpos[:, None], s, -1e30)
        p = jax.nn.softmax(s, axis=-1).astype(v.dtype)
        outs.append(jnp.einsum('bhqk,bkhd->bqhd', p, v[:, :kend]))
    o = jnp.concatenate(outs, axis=1)
    return o.reshape(B, S, MLA_HEADS * V_HEAD_DIM)


def setup_inputs(seed: int = 0) -> dict:
    key = jax.random.key(seed)
    ks = jax.random.split(key, 24)
    f32 = jnp.float32

    def nrm(k, shape, scale):
        return jax.random.normal(k, shape, f32) * scale

    def gain(k, shape):
        return 1.0 + 0.02 * jax.random.normal(k, shape, f32)

    x = jax.random.normal(ks[0], (BATCH, SEQ, D_MODEL), f32)
    c = jax.random.normal(ks[1], (BATCH, D_MODEL), f32)
    offset = jax.random.randint(ks[2], (BATCH, 1), 0, 1024, dtype=jnp.int32)
    positions = offset + jnp.arange(SEQ, dtype=jnp.int32)[None, :]
    return {
        "x": x,
        "c": c,
        "positions": positions,
        "w_ada": nrm(ks[3], (DEPTH, D_MODEL, N_MOD * D_MODEL), 0.02),
        "b_ada": nrm(ks[4], (DEPTH, N_MOD * D_MODEL), 0.01),
        "norm_mix_g": gain(ks[5], (DEPTH, D_MODEL)),
        "w_in": nrm(ks[6], (DEPTH, D_MODEL, D_IN), D_MODEL ** -0.5),
        "gmlp_ws": nrm(ks[7], (DEPTH, GMLP_GROUPS, CHUNK, CHUNK), CHUNK ** -0.5),
        "gmlp_bs": gain(ks[8], (DEPTH, GMLP_GROUPS, CHUNK)),
        "mla_q_norm_g": gain(ks[9], (DEPTH, Q_RANK)),
        "mla_kv_norm_g": gain(ks[10], (DEPTH, KV_RANK)),
        "mla_w_uq": nrm(ks[11], (DEPTH, Q_RANK, MLA_HEADS * (QK_NOPE_DIM + QK_ROPE_DIM)), Q_RANK ** -0.5),
        "mla_w_ukv": nrm(ks[12], (DEPTH, KV_RANK, MLA_HEADS * (QK_NOPE_DIM + V_HEAD_DIM)), KV_RANK ** -0.5),
        "out_norm_gmlp_g": gain(ks[13], (DEPTH, D_GMLP)),
        "out_norm_mla_g": gain(ks[14], (DEPTH, D_MLA)),
        "w_out": nrm(ks[15], (DEPTH, D_MIX, D_MODEL), D_MIX ** -0.5),
        "norm_ffn_g": gain(ks[16], (DEPTH, D_MODEL)),
        "w_ff1": nrm(ks[17], (DEPTH, D_MODEL, D_FF), D_MODEL ** -0.5),
        "w_ff2": nrm(ks[18], (DEPTH, D_FF, D_MODEL), D_FF ** -0.5),
        "final_norm_g": gain(ks[19], (D_MODEL,)),
    }


def reference(x, c, positions, w_ada, b_ada, norm_mix_g, w_in, gmlp_ws, gmlp_bs,
              mla_q_norm_g, mla_kv_norm_g, mla_w_uq, mla_w_ukv, out_norm_gmlp_g,
              out_norm_mla_g, w_out, norm_ffn_g, w_ff1, w_ff2, final_norm_g):
    B, S, _ = x.shape
    cos, sin = rope_tables(positions, QK_ROPE_DIM)
    c_act = jax.nn.silu(c)
    split_pts = [D_GMLP, 2 * D_GMLP, 2 * D_GMLP + Q_RANK, 2 * D_GMLP + Q_RANK + KV_RANK]
    for l in range(DEPTH):
        mod = c_act @ w_ada[l] + b_ada[l]
        shift1, scale1, gate1, shift2, scale2, gate2 = jnp.split(mod[:, None, :], N_MOD, axis=-1)

        h = rmsnorm(x, norm_mix_g[l]) * (1.0 + scale1) + shift1
        z = h @ w_in[l]
        u, v, q_lat, kv_lat, k_rope_raw = jnp.split(z, split_pts, axis=-1)
        y_g = gmlp_mixer(u.reshape(B, S, GMLP_GROUPS, GMLP_GROUP_DIM),
                         v.reshape(B, S, GMLP_GROUPS, GMLP_GROUP_DIM),
                         gmlp_ws[l], gmlp_bs[l]).reshape(B, S, D_GMLP)
        y_a = mla_mixer(q_lat, kv_lat, k_rope_raw, cos, sin, mla_q_norm_g[l],
                        mla_kv_norm_g[l], mla_w_uq[l], mla_w_ukv[l])
        y = jnp.concatenate([rmsnorm(y_g, out_norm_gmlp_g[l]), rmsnorm(y_a, out_norm_mla_g[l])], axis=-1)
        x = x + gate1 * (y @ w_out[l])

        h = rmsnorm(x, norm_ffn_g[l]) * (1.0 + scale2) + shift2
        f = jnp.square(jax.nn.relu(h @ w_ff1[l])) @ w_ff2[l]
        x = x + gate2 * f
    return rmsnorm(x, final_norm_g)
```

```python
import numpy as np
import concourse.bass as bass
import concourse.mybir as mybir
from concourse.bass_utils import run_bass_kernel_spmd

F32 = mybir.dt.float32
BF16 = mybir.dt.bfloat16
I32 = mybir.dt.int32
AF = mybir.ActivationFunctionType
ALU = mybir.AluOpType
AX = mybir.AxisListType

D = 1024
NCH = 8
DFF = 4096
EPS = 1e-6
NCORES = 8


class Buf:
    __slots__ = ("name", "lw", "rd", "psum")

    def __init__(self, name, psum=False, rd=None):
        self.name = name
        self.lw = None
        self.rd = list(rd) if rd else []
        self.psum = psum


class Chan:
    def __init__(self, sem):
        self.sem = sem
        self.cnt = 0


class Eng:
    def __init__(self, fw, name, is_pe=False):
        self.fw = fw
        self.name = name
        self.ops = []
        self.count = 0
        self.sem = None
        self.seen = {}
        self.is_pe = is_pe
        self.pend_r = []
        self.pend_w = []

    def _collect(self, reads, writes, is_dma=False, chan=None):
        waits = {}

        def need(m):
            k, v, en = m
            if v > waits.get(k, 0):
                waits[k] = v

        for b in reads:
            m = b.lw
            if m is not None:
                if m[2] == self.name and not is_dma:
                    if not self.is_pe:
                        need(m)
                else:
                    need(m)
            if b.psum:
                for r in b.rd:
                    if r[2] != self.name or is_dma:
                        need(r)
        for b in writes:
            m = b.lw
            if m is not None:
                if is_dma and chan is not None and m[0] == id(chan.sem):
                    pass
                elif m[2] == self.name and not is_dma:
                    pass
                else:
                    need(m)
            for r in b.rd:
                if r[2] != self.name or is_dma:
                    need(r)
        out = []
        for k, v in waits.items():
            if self.seen.get(k, 0) >= v:
                continue
            self.seen[k] = v
            out.append((self.fw.semobj[k], v))
        return out

    def op(self, fn, reads=(), writes=(), inc=True):
        waits = self._collect(reads, writes)
        self.pend_r.extend(reads)
        self.pend_w.extend(writes)
        if inc:
            self.count += 1
            mark = (id(self.sem), self.count, self.name)
            for b in self.pend_r:
                b.rd.append(mark)
            for b in self.pend_w:
                b.lw = mark
                b.rd = []
            self.pend_r = []
            self.pend_w = []
            self.ops.append((waits, fn, [(self.sem, 1)]))
        else:
            self.ops.append((waits, fn, []))

    def dma(self, fn, chan, reads=(), writes=()):
        waits = self._collect(reads, writes, is_dma=True, chan=chan)
        chan.cnt += 16
        mark = (id(chan.sem), chan.cnt, "dma")
        for b in reads:
            b.rd.append(mark)
        for b in writes:
            b.lw = mark
            b.rd = []
        self.ops.append((waits, fn, [(chan.sem, 16)]))

    def wait_for(self, bufs):
        waits = self._collect(bufs, ())
        self.ops.append((waits, None, []))


class FW:
    def __init__(self, nc):
        self.nc = nc
        self.semobj = {}
        self.chans = {}
        self.pe = Eng(self, "pe", is_pe=True)
        self.act = Eng(self, "act")
        self.dve = Eng(self, "dve")
        self.pool = Eng(self, "pool")
        self.sp = Eng(self, "sp")
        self.engs = [self.pe, self.act, self.dve, self.pool, self.sp]
        for e in self.engs:
            e.sem = self.new_sem("t_" + e.name)
        self.cur_barrier = []

    def new_sem(self, name):
        s = self.nc.alloc_semaphore(name)
        self.semobj[id(s)] = s
        return s

    def chan(self, name):
        if name not in self.chans:
            self.chans[name] = Chan(self.new_sem("d_" + name))
        return self.chans[name]

    def buf(self, name, psum=False, fresh=False):
        return Buf(name, psum=psum, rd=None if fresh else self.cur_barrier)

    def barrier(self):
        m = []
        for e in self.engs:
            if e.count > 0:
                m.append((id(e.sem), e.count, "barrier"))
        for c in self.chans.values():
            if c.cnt > 0:
                m.append((id(c.sem), c.cnt, "barrier"))
        self.cur_barrier = m

    @staticmethod
    def alias_into(dst_bufs, src_bufs):
        ms = []
        for b in src_bufs:
            if b.lw is not None:
                ms.append((b.lw[0], b.lw[1], "alias"))
            for r in b.rd:
                ms.append((r[0], r[1], "alias"))
        best = {}
        for k, v, _ in ms:
            if v > best.get(k, 0):
                best[k] = v
        ms = [(k, v, "alias") for k, v in best.items()]
        for d in dst_bufs:
            d.rd.extend(ms)

    def replay(self):
        nc = self.nc
        with nc.Block() as block:
            def run(rec):
                def body(e):
                    for waits, fn, incs in rec.ops:
                        for s, v in waits:
                            e.wait_ge(s, v)
                        if fn is None:
                            continue
                        ins = fn(e)
                        for s, v in incs:
                            ins.then_inc(s, v)
                return body
            block.tensor(run(self.pe))
            block.scalar(run(self.act))
            block.vector(run(self.dve))
            block.gpsimd(run(self.pool))
            block.sync(run(self.sp))


def MM(fw, out, lhsT, rhs, start, stop, r, w, inc):
    fw.pe.op(lambda e: e.matmul(out, lhsT=lhsT, rhs=rhs, start=start, stop=stop), r, w, inc)


def TR(fw, out, in_, ident, r, w, inc):
    fw.pe.op(lambda e: e.transpose(out, in_, ident), r, w, inc)


def ACTV(fw, out, in_, func, r, w, bias=None, scale=None, accum_out=None):
    kw = {}
    if bias is not None:
        kw["bias"] = bias
    if scale is not None:
        kw["scale"] = scale
    if accum_out is not None:
        kw["accum_out"] = accum_out
    fw.act.op(lambda e: e.activation(out=out, in_=in_, func=func, **kw), r, w)


def TT(eng, out, in0, in1, op, r, w):
    eng.op(lambda e: e.tensor_tensor(out=out, in0=in0, in1=in1, op=op), r, w)


def TS(eng, out, in0, s1, s2, op0, op1, r, w):
    if s2 is None:
        eng.op(lambda e: e.tensor_scalar(out=out, in0=in0, scalar1=s1, scalar2=None, op0=op0), r, w)
    else:
        eng.op(lambda e: e.tensor_scalar(out=out, in0=in0, scalar1=s1, scalar2=s2, op0=op0, op1=op1), r, w)


def STT(fw, out, in0, scalar, in1, op0, op1, r, w):
    fw.dve.op(lambda e: e.scalar_tensor_tensor(out=out, in0=in0, scalar=scalar, in1=in1, op0=op0, op1=op1), r, w)


def CP(eng, out, in_, r, w):
    eng.op(lambda e: e.tensor_copy(out=out, in_=in_), r, w)


def MS(eng, ap, val, r, w):
    eng.op(lambda e: e.memset(ap, val), r, w)


def RED(fw, out, in_, r, w):
    fw.dve.op(lambda e: e.tensor_reduce(out=out, in_=in_, axis=AX.X, op=ALU.add), r, w)


def RECIP(fw, out, in_, r, w):
    fw.dve.op(lambda e: e.reciprocal(out=out, in_=in_), r, w)


def DMA(eng, chan, out, in_, r, w):
    eng.dma(lambda e: e.dma_start(out=out, in_=in_), chan, r, w)


def build_program(S, NSEQ, LAYERS, FINAL_NORM, NLW):
    assert S % 512 == 0
    NG = S // 512
    NT = S // 128
    KB = NT
    nc = bass.Bass("TRN2", target_bir_lowering=False)
    fw = FW(nc)
    pe, act, dve, pool, sp = fw.pe, fw.act, fw.dve, fw.pool, fw.sp

    def din(name, shape, dt=F32):
        return nc.dram_tensor(name, shape, dt, kind="ExternalInput").ap()

    x_d = din("x", [NSEQ, S, D])
    crow_d = din("crows", [NSEQ * 8, 128])
    pos_d = din("pos", [NSEQ, S], I32)
    rc_d = din("ropec", [64, 4])
    w_ada = din("w_ada", [NLW, D, 6 * D])
    b_ada = din("b_ada", [NLW, 48, 128])
    g_mix = din("norm_mix_g", [NLW, 8, 128])
    w_in = din("w_in", [NLW, D, 1440])
    ws_d = din("gmlp_ws", [NLW, 8, 128, 128])
    bs_d = din("gmlp_bs", [NLW, 8, 128])
    gq_d = din("mla_q_norm_g", [NLW, 2, 128])
    gkv_d = din("mla_kv_norm_g", [NLW, 1, 128])
    w_uq = din("mla_w_uq", [NLW, 256, 768])
    w_ukv = din("mla_w_ukv", [NLW, 128, 1024])
    ggm_d = din("out_norm_gmlp_g", [NLW, 4, 128])
    gam_d = din("out_norm_mla_g", [NLW, 4, 128])
    w_out = din("w_out", [NLW, D, D])
    g_ffn = din("norm_ffn_g", [NLW, 8, 128])
    w_ff1 = din("w_ff1", [NLW, D, DFF])
    w_ff2 = din("w_ff2", [NLW, DFF, D])
    gf_d = din("final_norm_g", [8, 128])
    out_d = nc.dram_tensor("out", [NSEQ, S, D], F32, kind="ExternalOutput").ap()

    P_XT = 0
    P_HT = P_XT + 8 * S * 4
    P_TAB = P_HT + 8 * S * 2
    P_CONST = P_TAB + S * 4
    CONST_SZ = 4096
    V0 = P_CONST + CONST_SZ
    TOTAL = 212736
    STATIC_SZ = 7680 + 2048 + 4096 + 8192 + 2048
    ST0 = TOTAL - STATIC_SZ
    DA_SZ = ST0 - V0
    arena = nc.alloc_sbuf_tensor("arena", [128, TOTAL // 2], BF16)

    def ar(off, shape, dt):
        n = int(np.prod(shape))
        sz = n * (2 if dt == BF16 else 4)
        assert off % 4 == 0 and off + sz <= TOTAL, (off, sz)
        ap = arena[:, off // 2:(off + sz) // 2]
        if dt != BF16:
            ap = ap.bitcast(dt)
        if len(shape) == 2:
            ap = ap.rearrange("p (a b) -> p a b", a=shape[0])
        elif len(shape) == 3:
            ap = ap.rearrange("p (a b c) -> p a b c", a=shape[0], b=shape[1])
        return ap

    class DAlloc:
        def __init__(self, base, limit):
            self.base = base
            self.limit = limit
            self.cur = base

        def reset(self):
            self.cur = self.base

        def get(self, shape, dt):
            n = int(np.prod(shape)) * (2 if dt == BF16 else 4)
            n = (n + 31) // 32 * 32
            off = self.cur
            self.cur += n
            assert self.cur <= self.limit, ("DA overflow", self.cur - self.base, self.limit - self.base)
            return ar(off, shape, dt)

    da = DAlloc(V0, ST0)

    xT = ar(P_XT, [8, S], F32)
    hT = ar(P_HT, [8, S], BF16)
    tab = ar(P_TAB, [S], F32)
    c0 = P_CONST
    ident_f = ar(c0, [128], F32); c0 += 512
    ident_b = ar(c0, [128], BF16); c0 += 256
    mask_b = ar(c0, [128], BF16); c0 += 256
    ones_b = ar(c0, [128], BF16); c0 += 256
    NL = len(LAYERS)
    modc = ar(c0, [NL * NSEQ, 48], F32); c0 += NL * NSEQ * 48 * 4
    colsL = ar(c0, [NL, 96], F32); c0 += NL * 96 * 4
    colsM = ar(c0, [32], F32); c0 += 128
    cact = ar(c0, [NSEQ * 8], BF16); c0 += 64
    rcs = ar(c0, [4], F32); c0 += 16
    epsc = ar(c0, [1], F32); c0 += 16
    smalls = ar(c0, [64], F32); c0 += 256
    smalls2 = ar(c0, [64], F32); c0 += 256
    smalls3 = ar(c0, [64], F32); c0 += 256
    assert c0 <= P_CONST + CONST_SZ, c0 - P_CONST

    s0 = ST0
    w_inB = ar(s0, [8, 480], BF16); s0 += 7680
    w_ukvS = ar(s0, [1024], BF16); s0 += 2048
    w_uqS = ar(s0, [2, 8, 128], BF16); s0 += 4096
    w_outA = ar(s0, [4, 1024], BF16); s0 += 8192
    wsT = ar(s0, [8, 128], BF16); s0 += 2048

    B_winB = fw.buf("w_inB", fresh=True)
    B_wukv = fw.buf("w_ukv", fresh=True)
    B_wuq = fw.buf("w_uq", fresh=True)
    B_woutA = fw.buf("w_outA", fresh=True)
    B_wsT = fw.buf("wsT", fresh=True)

    pb = [nc.alloc_psum_tensor(f"pb{i}", [128, 512], F32) for i in range(8)]
    PB = [fw.buf(f"pb{i}", psum=True, fresh=True) for i in range(8)]

    B_xT = [[fw.buf(f"xT{m}_{g}", fresh=True) for g in range(NG)] for m in range(8)]
    B_hT = [[fw.buf(f"hT{k}_{g}", fresh=True) for g in range(NG)] for k in range(8)]
    B_tab = fw.buf("tab", fresh=True)
    B_const = fw.buf("const", fresh=True)
    B_modc = fw.buf("modc", fresh=True)
    B_cols = fw.buf("cols", fresh=True)
    B_out = fw.buf("out", fresh=True)
    ch_out = [fw.chan("out0"), fw.chan("out1")]

    def gc(g):
        return slice(g * 512, (g + 1) * 512)

    MS(pool, ident_f, 1.0, [], [B_const])
    pool.op(lambda e: e.affine_select(out=ident_f, in_=ident_f, pattern=[[-1, 128]], compare_op=ALU.is_equal,
                                      fill=0.0, base=0, channel_multiplier=1), [B_const], [B_const])
    CP(pool, ident_b, ident_f, [B_const], [B_const])
    MS(pool, ones_b, 1.0, [], [B_const])
    da.reset()
    mask_f = da.get([128], F32)
    B_mf = fw.buf("mask_f")
    MS(pool, mask_f, 1.0, [], [B_mf])
    pool.op(lambda e: e.affine_select(out=mask_f, in_=mask_f, pattern=[[1, 128]], compare_op=ALU.is_ge,
                                      fill=0.0, base=0, channel_multiplier=-1), [B_mf], [B_mf])
    CP(pool, mask_b, mask_f, [B_mf], [B_const])
    MS(pool, epsc, EPS, [], [B_const])
    DMA(sp, fw.chan("rc"), rcs[0:64, :], rc_d, [], [B_const])

    rows = [da.get([128], F32) for _ in range(NL + 1)]
    B_rows = [fw.buf(f"rows{i}") for i in range(NL + 1)]
    for i in range(NL + 1):
        MS(pool, rows[i], 0.0, [], [B_rows[i]])
    ch_rows_l = [fw.chan(f"rows{i}") for i in range(NL + 1)]
    for li, l in enumerate(LAYERS):
        rt = rows[li]
        for (r0, n, src) in [(0, 8, g_mix[l]), (8, 8, g_ffn[l]), (16, 4, ggm_d[l]), (20, 4, gam_d[l]),
                             (24, 2, gq_d[l]), (26, 1, gkv_d[l]), (27, 8, bs_d[l]), (35, 48, b_ada[l])]:
            DMA(sp, ch_rows_l[li], rt[r0:r0 + n, :], src, [], [B_rows[li]])
    DMA(sp, ch_rows_l[NL], rows[NL][0:NSEQ * 8, :], crow_d, [], [B_rows[NL]])
    DMA(sp, ch_rows_l[NL], rows[NL][16:24, :], gf_d, [], [B_rows[NL]])
    for i in range(NL + 1):
        TR(fw, pb[0][:, 0:128], rows[i], ident_f, [B_rows[i], B_const], [PB[0]], True)
        if i < NL:
            CP(dve, colsL[:, i, 0:96], pb[0][:, 0:96], [PB[0]], [B_cols])
        else:
            CP(dve, colsM[:, 0:32], pb[0][:, 0:32], [PB[0]], [B_cols])
    ACTV(fw, cact[:, :], colsM[:, 0:NSEQ * 8], AF.Silu, [B_cols], [B_const])

    def emit_norm(A_cols, B_cols_, tag):
        sq = da.get([8, 512], BF16)
        xn = [da.get([512], F32) for _ in range(2)]
        rstd = da.get([512], F32)
        B_sq = fw.buf("sq" + tag); B_xn = [fw.buf("xn0" + tag), fw.buf("xn1" + tag)]; B_rstd = fw.buf("rstd" + tag)
        for g in range(NG):
            for fc in range(8):
                ACTV(fw, sq[:, fc, :], xT[:, fc, gc(g)], AF.Square, [B_xT[fc][g]], [B_sq])
            for fc in range(8):
                MM(fw, pb[7][:, :], ones_b, sq[:, fc, :], fc == 0, fc == 7, [B_sq, B_const], [PB[7]], fc == 7)
            ACTV(fw, rstd, pb[7][:, :], AF.Ln, [PB[7], B_const], [B_rstd], bias=epsc[:, 0:1], scale=1.0 / D)
            ACTV(fw, rstd, rstd, AF.Exp, [B_rstd], [B_rstd], scale=-0.5)
            for fc in range(8):
                j = fc % 2
                TT(dve, xn[j], xT[:, fc, gc(g)], rstd, ALU.mult, [B_xT[fc][g], B_rstd], [B_xn[j]])
                if fc % 2 == 0:
                    TS(pool, hT[:, fc, gc(g)], xn[j], A_cols[:, fc:fc + 1], B_cols_[:, fc:fc + 1], ALU.mult, ALU.add,
                       [B_xn[j], B_modc], [B_hT[fc][g]])
                else:
                    ACTV(fw, hT[:, fc, gc(g)], xn[j], AF.Identity, [B_xn[j], B_modc], [B_hT[fc][g]],
                         bias=B_cols_[:, fc:fc + 1], scale=A_cols[:, fc:fc + 1])

    def emit_mod():
        wada = [da.get([8, 768], BF16) for _ in range(2)]
        B_wada = [fw.buf(f"wada{i}") for i in range(2)]
        ch_wada = [fw.chan("wada0"), fw.chan("wada1")]
        modT = da.get([48, NSEQ], F32)
        B_modT = fw.buf("modT")
        cact3 = cact.rearrange("p (s k) -> p s k", s=NSEQ)
        it = 0
        for li, l in enumerate(LAYERS):
            for cb in range(8):
                sl = it % 2
                it += 1
                src = w_ada[l][:, cb * 768:(cb + 1) * 768].rearrange("(k p) n -> p k n", p=128)
                DMA(pool, ch_wada[sl], wada[sl], src, [], [B_wada[sl]])
                for ch in range(6):
                    col0 = (cb * 6 + ch) * NSEQ
                    for k in range(8):
                        MM(fw, pb[1][:, col0:col0 + NSEQ], wada[sl][:, k, ch * 128:(ch + 1) * 128], cact3[:, :, k],
                           k == 0, k == 7, [B_wada[sl], B_const], [PB[1]], (k == 7 and ch == 5))
            TT(dve, modT, pb[1][:, 0:48 * NSEQ].rearrange("p (a b) -> p a b", b=NSEQ),
               colsL[:, li, 35:83].unsqueeze(2).broadcast_to([128, 48, NSEQ]), ALU.add, [PB[1], B_cols], [B_modT])
            for s in range(NSEQ):
                mc = modc[:, li * NSEQ + s, :]
                STT(fw, mc[:, 0:8], modT[:, 8:16, s], 1.0, colsL[:, li, 0:8], ALU.add, ALU.mult, [B_modT, B_cols], [B_modc])
                CP(dve, mc[:, 8:16], modT[:, 0:8, s], [B_modT], [B_modc])
                CP(dve, mc[:, 16:24], modT[:, 16:24, s], [B_modT], [B_modc])
                STT(fw, mc[:, 24:32], modT[:, 32:40, s], 1.0, colsL[:, li, 8:16], ALU.add, ALU.mult, [B_modT, B_cols], [B_modc])
                CP(dve, mc[:, 32:40], modT[:, 24:32, s], [B_modT], [B_modc])
                CP(dve, mc[:, 40:48], modT[:, 40:48, s], [B_modT], [B_modc])

    for s in range(NSEQ):
        if s > 0:
            fw.barrier()
            da.reset()
        xin = [da.get([1024], F32) for _ in range(2)]
        B_xin = [fw.buf("xin0"), fw.buf("xin1")]
        ch_xin = [fw.chan("xin0"), fw.chan("xin1")]
        posi = da.get([S], I32); posf = da.get([S], F32); kf = da.get([S], F32); ki = da.get([S], I32)
        B_pos = fw.buf("pos")
        for t in range(NT):
            j = t % 2
            DMA(sp, ch_xin[j], xin[j], x_d[s, t * 128:(t + 1) * 128, :], [], [B_xin[j]])
            for half in range(2):
                bank = (2 * t + half) % 4
                for q in range(4):
                    fc = half * 4 + q
                    TR(fw, pb[bank][:, q * 128:(q + 1) * 128], xin[j][:, fc * 128:(fc + 1) * 128], ident_f,
                       [B_xin[j], B_const], [PB[bank]], q == 3)
                eng = dve if half == 0 else None
                dst = xT[:, half * 4:half * 4 + 4, t * 128:(t + 1) * 128]
                src = pb[bank][:, :].rearrange("p (a b) -> p a b", a=4)
                wb = [B_xT[half * 4 + q][t // 4] for q in range(4)]
                if half == 0:
                    CP(dve, dst, src, [PB[bank]], wb)
                else:
                    ACTV(fw, dst, src, AF.Copy, [PB[bank]], wb)
        DMA(sp, fw.chan("pos"), posi[0:64, :], pos_d[s:s + 1, :].partition_broadcast(64), [], [B_pos])
        C1 = 6.28125
        C2 = 2 * np.pi - 6.28125
        P64 = slice(0, 64)
        CP(dve, posf[P64, :], posi[P64, :], [B_pos], [B_pos])
        TS(dve, posf[P64, :], posf[P64, :], rcs[P64, 0:1], None, ALU.mult, None, [B_pos, B_const], [B_pos])
        TS(dve, kf[P64, :], posf[P64, :], 1.0 / (2 * np.pi), None, ALU.mult, None, [B_pos], [B_pos])
        CP(dve, ki[P64, :], kf[P64, :], [B_pos], [B_pos])
        CP(dve, kf[P64, :], ki[P64, :], [B_pos], [B_pos])
        STT(fw, posf[P64, :], kf[P64, :], -C1, posf[P64, :], ALU.mult, ALU.add, [B_pos], [B_pos])
        STT(fw, posf[P64, :], kf[P64, :], -C2, posf[P64, :], ALU.mult, ALU.add, [B_pos], [B_pos])
        TS(dve, posf[P64, :], posf[P64, :], rcs[P64, 1:2], None, ALU.add, None, [B_pos, B_const], [B_pos])
        TS(dve, kf[P64, :], posf[P64, :], float(np.pi), float(-2 * np.pi), ALU.is_gt, ALU.mult, [B_pos], [B_pos])
        TT(dve, posf[P64, :], posf[P64, :], kf[P64, :], ALU.add, [B_pos], [B_pos])
        ACTV(fw, tab[P64, :], posf[P64, :], AF.Sin, [B_pos, B_const], [B_tab], scale=rcs[P64, 2:3])

        if s == 0:
            emit_mod()

        for li, l in enumerate(LAYERS):
            mc = modc[:, li * NSEQ + s, :]
            A1, B1, G1 = mc[:, 0:8], mc[:, 8:16], mc[:, 16:24]
            A2, B2, G2 = mc[:, 24:32], mc[:, 32:40], mc[:, 40:48]
            cl = colsL[:, li, :]
            ggm, gam, gq, gkv, bsc = cl[:, 16:20], cl[:, 20:24], cl[:, 24:26], cl[:, 26:27], cl[:, 27:35]

            DMA(pool, fw.chan("w_inB"), w_inB[:, :, 0:416],
                w_in[l][:, 1024:1440].rearrange("(k p) n -> p k n", p=128), [], [B_winB])
            DMA(pool, fw.chan("w_inB"), w_inB[:, :, 416:432],
                w_in[l][:, 1424:1440].rearrange("(k p) n -> p k n", p=128), [], [B_winB])
            DMA(pool, fw.chan("w_inB"), w_inB[:, :, 432:448],
                w_in[l][:, 1408:1424].rearrange("(k p) n -> p k n", p=128), [], [B_winB])
            ukv_src = w_ukv[l].rearrange("p (h two d) -> p two h d", h=8, two=2)
            ukv_dst = w_ukvS.rearrange("p (two h d) -> p two h d", two=2, h=8)
            for two in range(2):
                DMA(pool, fw.chan("w_ukv"), ukv_dst[:, two, :, :], ukv_src[:, two, :, :], [], [B_wukv])
            uq_src = w_uq[l].rearrange("(c p) (h d) -> p c h d", p=128, d=96)
            for (d0, d1, s0_, s1_) in [(0, 32, 64, 96), (32, 48, 80, 96), (48, 64, 64, 80), (64, 128, 0, 64)]:
                for c in range(2):
                    DMA(pool, fw.chan("w_uq"), w_uqS[:, c, :, d0:d1], uq_src[:, c, :, s0_:s1_], [], [B_wuq])
            DMA(pool, fw.chan("w_outA"), w_outA, w_out[l][512:1024, :].rearrange("(c p) n -> p c n", p=128),
                [], [B_woutA])

            fw.barrier()
            da.reset()
            w_inA = da.get([8, 1024], BF16); B_winA = fw.buf("w_inA")
            w_outG = da.get([4, 1024], BF16); B_woutG = fw.buf("w_outG")
            ws_st = da.get([8, 128], F32); B_wsst = fw.buf("ws_st")
            DMA(pool, fw.chan("w_inA"), w_inA, w_in[l][:, 0:1024].rearrange("(k p) n -> p k n", p=128), [], [B_winA])
            DMA(pool, fw.chan("w_outG"), w_outG, w_out[l][0:512, :].rearrange("(c p) n -> p c n", p=128), [], [B_woutG])
            DMA(sp, fw.chan("ws_st"), ws_st, ws_d[l].rearrange("g t s -> t g s"), [], [B_wsst])
            for g8 in range(8):
                bank = g8 // 4
                TR(fw, pb[bank][:, (g8 % 4) * 128:(g8 % 4 + 1) * 128], ws_st[:, g8, :], ident_f,
                   [B_wsst, B_const], [PB[bank]], g8 % 4 == 3)
            for bank in range(2):
                TT(dve, wsT[:, bank * 4:bank * 4 + 4, :], pb[bank][:, :].rearrange("p (a b) -> p a b", a=4),
                   mask_b.unsqueeze(1).broadcast_to([128, 4, 128]), ALU.mult, [PB[bank], B_const], [B_wsT])

            emit_norm(A1, B1, "n1")

            yTg = [da.get([4, 512], BF16) for _ in range(2)]
            B_yTg = [fw.buf("yTg0"), fw.buf("yTg1")]
            R3 = 3
            gu = [da.get([512], BF16) for _ in range(R3)]; B_gu = [fw.buf(f"gu{i}") for i in range(R3)]
            gv = [da.get([512], F32) for _ in range(R3)]; B_gv = [fw.buf(f"gv{i}") for i in range(R3)]
            sqt = [da.get([512], F32) for _ in range(R3)]; B_sqt = [fw.buf(f"sqt{i}") for i in range(R3)]
            vn = [da.get([512], BF16) for _ in range(R3)]; B_vn = [fw.buf(f"vn{i}") for i in range(R3)]
            ygn = [da.get([512], BF16) for _ in range(R3)]; B_ygn = [fw.buf(f"ygn{i}") for i in range(R3)]
            st = [smalls, smalls2, smalls3]; B_st = [fw.buf(f"st{i}") for i in range(R3)]

            def g_s0(t):
                j = t % 2
                g = t // 4
                tc_ = slice(t * 128, (t + 1) * 128)
                for k in range(8):
                    MM(fw, pb[0 + j][:, :], hT[:, k, tc_], w_inA[:, k, 0:512], k == 0, k == 7,
                       [B_hT[k][g], B_winA], [PB[0 + j]], k == 7)
                    if k % 2 == 1:
                        yield
                for k in range(8):
                    MM(fw, pb[2 + j][:, :], hT[:, k, tc_], w_inA[:, k, 512:1024], k == 0, k == 7,
                       [B_hT[k][g], B_winA], [PB[2 + j]], k == 7)
                    if k % 2 == 1:
                        yield

            def g_s1(t):
                j = t % 2
                r = t % R3
                s_ = st[r]
                ACTV(fw, gu[r], pb[0 + j][:, :], AF.Gelu_apprx_tanh, [PB[0 + j]], [B_gu[r]])
                yield
                ACTV(fw, gv[r], pb[2 + j][:, :], AF.Gelu_apprx_tanh, [PB[2 + j]], [B_gv[r]])
                yield
                gv3 = gv[r].rearrange("p (g d) -> p g d", g=8)
                sq3 = sqt[r].rearrange("p (g d) -> p g d", g=8)
                TT(pool, sqt[r], gv[r], gv[r], ALU.mult, [B_gv[r]], [B_sqt[r]])
                yield
                RED(fw, s_[:, 0:8], gv3, [B_gv[r]], [B_st[r]])
                yield
                RED(fw, s_[:, 8:16], sq3, [B_sqt[r]], [B_st[r]])
                yield
                TS(dve, s_[:, 0:8], s_[:, 0:8], 1.0 / 64, None, ALU.mult, None, [B_st[r]], [B_st[r]])
                yield
                TT(dve, s_[:, 16:24], s_[:, 0:8], s_[:, 0:8], ALU.mult, [B_st[r]], [B_st[r]])
                yield
                STT(fw, s_[:, 8:16], s_[:, 8:16], 1.0 / 64, s_[:, 16:24], ALU.mult, ALU.subtract, [B_st[r]], [B_st[r]])
                yield
                ACTV(fw, s_[:, 8:16], s_[:, 8:16], AF.Ln, [B_st[r], B_const], [B_st[r]], bias=epsc[:, 0:1])
                yield
                ACTV(fw, s_[:, 8:16], s_[:, 8:16], AF.Exp, [B_st[r]], [B_st[r]], scale=-0.5)
                yield

            def g_s2(t):
                j = t % 2
                r = t % R3
                s_ = st[r]
                gv3 = gv[r].rearrange("p (g d) -> p g d", g=8)
                sq3 = sqt[r].rearrange("p (g d) -> p g d", g=8)
                TT(dve, sq3, gv3, s_[:, 0:8].unsqueeze(2).broadcast_to([128, 8, 64]), ALU.subtract,
                   [B_gv[r], B_st[r]], [B_sqt[r]])
                yield
                TT(pool, vn[r].rearrange("p (g d) -> p g d", g=8), sq3,
                   s_[:, 8:16].unsqueeze(2).broadcast_to([128, 8, 64]), ALU.mult, [B_sqt[r], B_st[r]], [B_vn[r]])
                yield
                for g8 in range(8):
                    MM(fw, pb[4 + j][:, g8 * 64:(g8 + 1) * 64], wsT[:, g8, :], vn[r][:, g8 * 64:(g8 + 1) * 64],
                       True, True, [B_wsT, B_vn[r]], [PB[4 + j]], g8 == 7)
                    if g8 % 2 == 1:
                        yield
                TT(dve, sq3, pb[4 + j][:, :].rearrange("p (g d) -> p g d", g=8),
                   bsc.unsqueeze(2).broadcast_to([128, 8, 64]), ALU.add, [PB[4 + j], B_cols], [B_sqt[r]])
                yield
                TT(pool, gv[r], sqt[r], gu[r], ALU.mult, [B_sqt[r], B_gu[r]], [B_gv[r]])
                yield
                MS(dve, s_[:, 24:25], 0.0, [], [B_st[r]])
                yield
                ACTV(fw, sqt[r], gv[r], AF.Square, [B_gv[r], B_st[r]], [B_sqt[r], B_st[r]], accum_out=s_[:, 24:25])
                yield
                ACTV(fw, s_[:, 24:25], s_[:, 24:25], AF.Ln, [B_st[r], B_const], [B_st[r]], bias=epsc[:, 0:1], scale=1.0 / 512)
                yield
                ACTV(fw, s_[:, 24:25], s_[:, 24:25], AF.Exp, [B_st[r]], [B_st[r]], scale=-0.5)
                yield

            def g_s3(t):
                r = t % R3
                g = t // 4
                s_ = st[r]
                TS(dve, ygn[r], gv[r], s_[:, 24:25], None, ALU.mult, None, [B_gv[r], B_st[r]], [B_ygn[r]])
                yield
                ptr = pb[6][:, :].bitcast(BF16)
                for c in range(4):
                    TR(fw, ptr[:, c * 128:(c + 1) * 128], ygn[r][:, c * 128:(c + 1) * 128], ident_b,
                       [B_ygn[r], B_const], [PB[6]], c == 3)
                    if c % 2 == 1:
                        yield
                TT(dve, yTg[g % 2][:, :, (t % 4) * 128:(t % 4 + 1) * 128],
                   ptr[:, 0:512].rearrange("p (a b) -> p a b", a=4),
                   ggm.unsqueeze(2).broadcast_to([128, 4, 128]), ALU.mult, [PB[6], B_cols], [B_yTg[g % 2]])
                yield

            def g_s4(g):
                for m in range(8):
                    for c in range(4):
                        MM(fw, pb[7][:, :], w_outG[:, c, m * 128:(m + 1) * 128], yTg[g % 2][:, c, :],
                           c == 0, c == 3, [B_woutG, B_yTg[g % 2]], [PB[7]], c == 3)
                    yield
                    STT(fw, xT[:, m, gc(g)], pb[7][:, :], G1[:, m:m + 1], xT[:, m, gc(g)], ALU.mult, ALU.add,
                        [PB[7], B_modc, B_xT[m][g]], [B_xT[m][g]])
                    yield

            def round_robin(gens):
                gens = [g_ for g_ in gens if g_ is not None]
                while gens:
                    nxt = []
                    for g_ in gens:
                        try:
                            next(g_)
                            nxt.append(g_)
                        except StopIteration:
                            pass
                    gens = nxt

            for step in range(NT + 5):
                gl = []
                if step < NT:
                    gl.append(g_s0(step))
                if 0 <= step - 1 < NT:
                    gl.append(g_s1(step - 1))
                if 0 <= step - 2 < NT:
                    gl.append(g_s2(step - 2))
                if 0 <= step - 3 < NT:
                    gl.append(g_s3(step - 3))
                t4 = step - 4
                if 0 <= t4 < NT and t4 % 4 == 3:
                    gl.append(g_s4(t4 // 4))
                round_robin(gl)

            fw.barrier()
            da.reset()
            kT = da.get([8, S], BF16)
            vaug = da.get([KB, 4, 192], BF16)
            cqT = da.get([2, S], BF16)
            sqm = da.get([3, 512], BF16)
            rst = da.get([512], F32)
            ckv = [da.get([512], BF16) for _ in range(2)]
            t1 = da.get([512], F32)
            t2 = da.get([512], F32)
            B_kT = [[fw.buf(f"kT{h}_{g}") for g in range(NG)] for h in range(8)]
            B_kz = fw.buf("kz")
            B_vaug = [fw.buf(f"vaug{j}") for j in range(KB)]
            B_cq = [fw.buf(f"cq{g}") for g in range(NG)]
            B_sqm = fw.buf("sqm"); B_rst = fw.buf("rst"); B_ckv = [fw.buf("ckv0"), fw.buf("ckv1")]
            B_t1 = fw.buf("t1"); B_t2 = fw.buf("t2")
            MS(pool, kT[32:64, :, :], 0.0, [], [B_kz])
            for j in range(KB):
                MS(pool, vaug[:, j, :, 64:128], 1.0, [], [B_vaug[j]])
            for g in range(NG):
                cols = [(0, 0, 128), (1, 128, 256), (2, 256, 384)]
                for (bank, c0_, c1_) in cols:
                    for k in range(8):
                        MM(fw, pb[bank][:, :], w_inB[:, k, c0_:c1_], hT[:, k, gc(g)], k == 0, k == 7,
                           [B_winB, B_hT[k][g]], [PB[bank]], k == 7)
                for k in range(8):
                    MM(fw, pb[3][0:64, :], w_inB[:, k, 384:448], hT[:, k, gc(g)], k == 0, k == 7,
                       [B_winB, B_hT[k][g]], [PB[3]], k == 7)
                for i in range(3):
                    ACTV(fw, sqm[:, i, :], pb[i][:, :], AF.Square, [PB[i]], [B_sqm])
                MM(fw, pb[4][:, :], ones_b, sqm[:, 0, :], True, False, [B_sqm, B_const], [PB[4]], False)
                MM(fw, pb[4][:, :], ones_b, sqm[:, 1, :], False, True, [B_sqm, B_const], [PB[4]], True)
                MM(fw, pb[5][:, :], ones_b, sqm[:, 2, :], True, True, [B_sqm, B_const], [PB[5]], True)
                ACTV(fw, rst, pb[4][:, :], AF.Ln, [PB[4], B_const], [B_rst], bias=epsc[:, 0:1], scale=1.0 / 256)
                ACTV(fw, rst, rst, AF.Exp, [B_rst], [B_rst], scale=-0.5)
                for c in range(2):
                    STT(fw, cqT[:, c, gc(g)], pb[c][:, :], gq[:, c:c + 1], rst, ALU.mult, ALU.mult,
                        [PB[c], B_cols, B_rst], [B_cq[g]])
                ACTV(fw, rst, pb[5][:, :], AF.Ln, [PB[5], B_const], [B_rst], bias=epsc[:, 0:1], scale=1.0 / 128)
                ACTV(fw, rst, rst, AF.Exp, [B_rst], [B_rst], scale=-0.5)
                cj = g % 2
                STT(fw, ckv[cj], pb[2][:, :], gkv[:, 0:1], rst, ALU.mult, ALU.mult, [PB[2], B_cols, B_rst], [B_ckv[cj]])
                TT(dve, t1[0:32, :], pb[3][0:32, :], tab[0:32, gc(g)], ALU.mult, [PB[3], B_tab], [B_t1])
                TT(dve, t2[0:32, :], pb[3][32:64, :], tab[32:64, gc(g)], ALU.mult, [PB[3], B_tab], [B_t2])
                TT(pool, kT[0:32, :, gc(g)], t1[0:32, :].unsqueeze(1).broadcast_to([32, 8, 512]),
                   t2[0:32, :].unsqueeze(1).broadcast_to([32, 8, 512]), ALU.add, [B_t1, B_t2],
                   [B_kT[h][g] for h in range(8)])
                for h in range(8):
                    bank = 6 + (h % 2)
                    MM(fw, pb[bank][64:128, :], w_ukvS[:, h * 64:(h + 1) * 64], ckv[cj], True, True,
                       [B_wukv, B_ckv[cj]], [PB[bank]], True)
                    if h % 2 == 0:
                        ACTV(fw, kT[64:128, h, gc(g)], pb[bank][64:128, :], AF.Copy, [PB[bank]], [B_kT[h][g]])
                    else:
                        CP(dve, kT[64:128, h, gc(g)], pb[bank][64:128, :], [PB[bank]], [B_kT[h][g]])
                for tt_ in range(4):
                    t = g * 4 + tt_
                    bank = 4 + (tt_ % 2)
                    MM(fw, pb[bank][:, :], ckv[cj][:, tt_ * 128:(tt_ + 1) * 128], w_ukvS[:, 512:1024], True, True,
                       [B_ckv[cj], B_wukv], [PB[bank]], True)
                    dst = vaug[:, t, :, :].rearrange("p a (b c) -> p a b c", b=3)[:, :, 0:3:2, :]
                    src = pb[bank][:, :].rearrange("p (a b c) -> p a b c", a=4, b=2)
                    if tt_ % 2 == 0:
                        ACTV(fw, dst, src, AF.Copy, [PB[bank]], [B_vaug[t]])
                    else:
                        CP(dve, dst, src, [PB[bank]], [B_vaug[t]])

            hoff = [P_HT]

            def hget(shape, dt):
                if S < 2048:
                    return da.get(shape, dt)
                n = int(np.prod(shape)) * (2 if dt == BF16 else 4)
                n = (n + 31) // 32 * 32
                off = hoff[0]
                hoff[0] += n
                assert hoff[0] <= P_HT + 8 * S * 2
                return ar(off, shape, dt)

            NQR = 3
            qTr = [hget([512], BF16) for _ in range(NQR)]
            o_n = hget([4, 512], F32)
            rr = [hget([512], F32) for _ in range(2)]
            rr2 = [hget([512], F32) for _ in range(2)]
            pT = [hget([512], BF16) for _ in range(3)]
            yTa = hget([4, 512], BF16)
            rsty = hget([512], F32)
            rt1 = hget([512], F32)
            rt2 = hget([512], F32)
            allh = [B_hT[k][g] for k in range(8) for g in range(NG)]
            scr = []

            def sbuf_(name):
                b = fw.buf(name, fresh=True)
                FW.alias_into([b], allh)
                scr.append(b)
                return b

            B_qT = [sbuf_(f"qT{i}") for i in range(NQR)]
            B_on = [sbuf_(f"on{c}") for c in range(4)]
            B_rr = [sbuf_("rr0"), sbuf_("rr1")]
            B_rr2 = [sbuf_("rr20"), sbuf_("rr21")]
            B_pT = [sbuf_(f"pT{i}") for i in range(3)]
            B_yTa = sbuf_("yTa")
            B_rsty = sbuf_("rsty")
            B_rt1 = sbuf_("rt1"); B_rt2 = sbuf_("rt2")
            for i in range(NQR):
                MS(pool, qTr[i][32:64, :], 0.0, [], [B_qT[i]])

            SCALE = float(96 ** -0.5)
            units = [(Q, h) for Q in range(NG) for h in range(8)]

            def emit_q(u):
                Q, h = units[u]
                slot = u % NQR
                bank = u % 2
                for c in range(2):
                    MM(fw, pb[bank][:, :], w_uqS[:, c, h, :], cqT[:, c, gc(Q)], c == 0, c == 1,
                       [B_wuq, B_cq[Q]], [PB[bank]], c == 1)
                CP(dve, qTr[slot][64:128, :], pb[bank][64:128, :], [PB[bank]], [B_qT[slot]])
                TT(dve, rt1[0:32, :], pb[bank][0:32, :], tab[0:32, gc(Q)], ALU.mult, [PB[bank], B_tab], [B_rt1])
                TT(dve, rt2[0:32, :], pb[bank][32:64, :], tab[32:64, gc(Q)], ALU.mult, [PB[bank], B_tab], [B_rt2])
                TT(pool, qTr[slot][0:32, :], rt1[0:32, :], rt2[0:32, :], ALU.add, [B_rt1, B_rt2], [B_qT[slot]])

            jobs = []
            for u, (Q, h) in enumerate(units):
                for j in range(4 * Q + 4):
                    jobs.append((u, j))

            def emit_qk(n):
                u, j = jobs[n]
                Q, h = units[u]
                qoff = max(0, (j - 4 * Q) * 128)
                sb = 2 + (n % 3)
                MM(fw, pb[sb][:, qoff:512], kT[:, h, j * 128:(j + 1) * 128], qTr[u % NQR][:, qoff:512], True, True,
                   [B_kT[h][j // 4], B_kz, B_qT[u % NQR]], [PB[sb]], True)
                ps_ = n % 3
                ACTV(fw, pT[ps_][:, qoff:512], pb[sb][:, qoff:512], AF.Exp, [PB[sb]], [B_pT[ps_]], scale=SCALE)
                if j >= 4 * Q:
                    TT(pool, pT[ps_][:, qoff:qoff + 128], pT[ps_][:, qoff:qoff + 128], mask_b, ALU.mult,
                       [B_pT[ps_], B_const], [B_pT[ps_]])

            def emit_pv(n):
                u, j = jobs[n]
                Q, h = units[u]
                qoff = max(0, (j - 4 * Q) * 128)
                p = h // 2
                ob = 5 + (u % 2)
                last = (j == 4 * Q + 3)
                stat = vaug[:, j, p, 0:128] if h % 2 == 0 else vaug[:, j, p, 64:192]
                MM(fw, pb[ob][:, qoff:512], stat, pT[n % 3][:, qoff:512], j == 0, last,
                   [B_vaug[j], B_pT[n % 3]], [PB[ob]], last)
                if last:
                    if h % 2 == 0:
                        orow, rrow = slice(0, 64), slice(64, 128)
                    else:
                        orow, rrow = slice(64, 128), slice(0, 64)
                    k2 = u % 2
                    RECIP(fw, rr[k2][rrow, :], pb[ob][rrow, :], [PB[ob]], [B_rr[k2]])
                    CP(dve, rr2[k2][orow, :], rr[k2][rrow, :], [B_rr[k2]], [B_rr2[k2]])
                    TT(dve, o_n[orow, p, :], pb[ob][orow, :], rr2[k2][orow, :], ALU.mult, [PB[ob], B_rr2[k2]], [B_on[p]])

            def emit_group_stats(Q):
                for c in range(4):
                    TT(pool, yTa[:, c, :], o_n[:, c, :], o_n[:, c, :], ALU.mult, [B_on[c]], [B_yTa])
                for c in range(4):
                    MM(fw, pb[7][:, :], ones_b, yTa[:, c, :], c == 0, c == 3, [B_yTa, B_const], [PB[7]], c == 3)
                ACTV(fw, rsty, pb[7][:, :], AF.Ln, [PB[7], B_const], [B_rsty], bias=epsc[:, 0:1], scale=1.0 / 512)
                ACTV(fw, rsty, rsty, AF.Exp, [B_rsty], [B_rsty], scale=-0.5)
                for c in range(4):
                    STT(fw, yTa[:, c, :], o_n[:, c, :], gam[:, c:c + 1], rsty, ALU.mult, ALU.mult,
                        [B_on[c], B_cols, B_rsty], [B_yTa])

            def emit_group_out(Q):
                for m in range(8):
                    for c in range(4):
                        MM(fw, pb[7][:, :], w_outA[:, c, m * 128:(m + 1) * 128], yTa[:, c, :], c == 0, c == 3,
                           [B_woutA, B_yTa], [PB[7]], c == 3)
                    STT(fw, xT[:, m, gc(Q)], pb[7][:, :], G1[:, m:m + 1], xT[:, m, gc(Q)], ALU.mult, ALU.add,
                        [PB[7], B_modc, B_xT[m][Q]], [B_xT[m][Q]])

            NJ = len(jobs)
            emit_q(0)
            first_job_of = {}
            last_job_of = {}
            for n, (u, j) in enumerate(jobs):
                first_job_of.setdefault(u, n)
                last_job_of[u] = n
            LOOK = 2
            for n in range(NJ + LOOK):
                if n < NJ:
                    u, j = jobs[n]
                    if n == first_job_of[u] and u + 1 < len(units):
                        emit_q(u + 1)
                    emit_qk(n)
                if n >= LOOK:
                    m_ = n - LOOK
                    emit_pv(m_)
                    u, j = jobs[m_]
                    Q, h = units[u]
                    if m_ == last_job_of[u] and h == 7:
                        emit_group_stats(Q)
                        emit_group_out(Q)
            FW.alias_into(allh, scr)

            fw.barrier()
            da.reset()
            emit_norm(A2, B2, "n2")
            NCG = 8
            RING = 3
            w1r = [da.get([8, 512], BF16) for _ in range(RING)]
            w2r = [da.get([4, 1024], BF16) for _ in range(RING)]
            B_w1 = [fw.buf(f"w1_{i}") for i in range(RING)]
            B_w2 = [fw.buf(f"w2_{i}") for i in range(RING)]
            aT = [da.get([4, 512], BF16) for _ in range(2)]
            B_aT = [fw.buf("aT0"), fw.buf("aT1")]
            rl = [da.get([512], F32) for _ in range(3)]
            B_rl = [fw.buf(f"rl{i}") for i in range(3)]

            def ffn_load(cg):
                sl = cg % RING
                DMA(pool, fw.chan(f"w1_{sl}"), w1r[sl],
                    w_ff1[l][:, cg * 512:(cg + 1) * 512].rearrange("(k p) n -> p k n", p=128), [], [B_w1[sl]])
                DMA(pool, fw.chan(f"w2_{sl}"), w2r[sl],
                    w_ff2[l][cg * 512:(cg + 1) * 512, :].rearrange("(c p) n -> p c n", p=128), [], [B_w2[sl]])

            fjobs = [(cg, g) for cg in range(NCG) for g in range(NG)]
            rlc = [0]

            def ff1(n):
                cg, g = fjobs[n]
                sl = cg % RING
                a = n % 2
                for c in range(4):
                    bank = (n * 4 + c) % 4
                    for k in range(8):
                        MM(fw, pb[bank][:, :], w1r[sl][:, k, c * 128:(c + 1) * 128], hT[:, k, gc(g)], k == 0, k == 7,
                           [B_w1[sl], B_hT[k][g]], [PB[bank]], k == 7)
                    ri = rlc[0] % 3
                    rlc[0] += 1
                    ACTV(fw, rl[ri], pb[bank][:, :], AF.Relu, [PB[bank]], [B_rl[ri]])
                    TT(pool, aT[a][:, c, :], rl[ri], rl[ri], ALU.mult, [B_rl[ri]], [B_aT[a]])

            def ff2(n):
                cg, g = fjobs[n]
                sl = cg % RING
                a = n % 2
                for m in range(8):
                    bank = 4 + (m % 4)
                    for c in range(4):
                        MM(fw, pb[bank][:, :], w2r[sl][:, c, m * 128:(m + 1) * 128], aT[a][:, c, :], c == 0, c == 3,
                           [B_w2[sl], B_aT[a]], [PB[bank]], c == 3)
                    STT(fw, xT[:, m, gc(g)], pb[bank][:, :], G2[:, m:m + 1], xT[:, m, gc(g)], ALU.mult, ALU.add,
                        [PB[bank], B_modc, B_xT[m][g]], [B_xT[m][g]])

            ffn_load(0)
            ffn_load(1)
            for n in range(len(fjobs) + 1):
                if n < len(fjobs):
                    ff1(n)
                if n >= 1:
                    ff2(n - 1)
                if n < len(fjobs):
                    cg, g = fjobs[n]
                    if g == 0 and cg + 2 < NCG:
                        ffn_load(cg + 2)

        fw.barrier()
        da.reset()
        xo = [da.get([1024], F32) for _ in range(2)]
        B_xo = [fw.buf("xo0"), fw.buf("xo1")]
        if FINAL_NORM:
            sq = da.get([8, 512], BF16); B_sq = fw.buf("fsq")
            rstd = da.get([512], F32); B_rstd = fw.buf("frstd")
            yt = da.get([8, 512], F32); B_yt = fw.buf("fyt")
            gfc = colsM[:, 16:24]
        for g in range(NG):
            if FINAL_NORM:
                for fc in range(8):
                    ACTV(fw, sq[:, fc, :], xT[:, fc, gc(g)], AF.Square, [B_xT[fc][g]], [B_sq])
                for fc in range(8):
                    MM(fw, pb[7][:, :], ones_b, sq[:, fc, :], fc == 0, fc == 7, [B_sq, B_const], [PB[7]], fc == 7)
                ACTV(fw, rstd, pb[7][:, :], AF.Ln, [PB[7], B_const], [B_rstd], bias=epsc[:, 0:1], scale=1.0 / D)
                ACTV(fw, rstd, rstd, AF.Exp, [B_rstd], [B_rstd], scale=-0.5)
                for fc in range(8):
                    STT(fw, yt[:, fc, :], xT[:, fc, gc(g)], gfc[:, fc:fc + 1], rstd, ALU.mult, ALU.mult,
                        [B_xT[fc][g], B_cols, B_rstd], [B_yt])
            for tt_ in range(4):
                t = g * 4 + tt_
                j = t % 2
                for half in range(2):
                    bank = (2 * t + half) % 4
                    for q in range(4):
                        fc = half * 4 + q
                        if FINAL_NORM:
                            src = yt[:, fc, tt_ * 128:(tt_ + 1) * 128]
                            rb = [B_yt, B_const]
                        else:
                            src = xT[:, fc, t * 128:(t + 1) * 128]
                            rb = [B_xT[fc][g], B_const]
                        TR(fw, pb[bank][:, q * 128:(q + 1) * 128], src, ident_f, rb, [PB[bank]], q == 3)
                    if half == 0:
                        CP(dve, xo[j][:, 0:512], pb[bank][:, :], [PB[bank]], [B_xo[j]])
                    else:
                        ACTV(fw, xo[j][:, 512:1024], pb[bank][:, :], AF.Copy, [PB[bank]], [B_xo[j]])
                DMA(sp, ch_out[j], out_d[s, t * 128:(t + 1) * 128, :], xo[j], [B_xo[j]], [B_out])

    for c in ch_out:
        if c.cnt > 0:
            sem, cnt = c.sem, c.cnt
            sp.ops.append(([(sem, cnt)], None, []))
    fw.replay()
    return nc


def _rope_consts():
    rc = np.zeros((64, 4), np.float32)
    freqs = (10000.0 ** (-np.arange(0, 32, 2, dtype=np.float32) / np.float32(32))).astype(np.float32)
    for p in range(64):
        rc[p, 0] = freqs[p % 16]
    rc[0:32, 1] = np.pi / 2
    rc[0:32, 2] = 1.0
    rc[32:48, 2] = -1.0
    rc[48:64, 2] = 1.0
    return rc


_PROG_CACHE = {}


def _get_prog(key):
    if key not in _PROG_CACHE:
        _PROG_CACHE[key] = build_program(*key)
    return _PROG_CACHE[key]


def _weights_map(inp, nlw):
    f = lambda a: np.ascontiguousarray(np.asarray(a, dtype=np.float32))
    return {
        "w_ada": f(inp["w_ada"]),
        "b_ada": f(inp["b_ada"]).reshape(nlw, 48, 128),
        "norm_mix_g": f(inp["norm_mix_g"]).reshape(nlw, 8, 128),
        "w_in": f(inp["w_in"]),
        "gmlp_ws": f(inp["gmlp_ws"]),
        "gmlp_bs": f(inp["gmlp_bs"]),
        "mla_q_norm_g": f(inp["mla_q_norm_g"]).reshape(nlw, 2, 128),
        "mla_kv_norm_g": f(inp["mla_kv_norm_g"]).reshape(nlw, 1, 128),
        "mla_w_uq": f(inp["mla_w_uq"]),
        "mla_w_ukv": f(inp["mla_w_ukv"]),
        "out_norm_gmlp_g": f(inp["out_norm_gmlp_g"]).reshape(nlw, 4, 128),
        "out_norm_mla_g": f(inp["out_norm_mla_g"]).reshape(nlw, 4, 128),
        "w_out": f(inp["w_out"]),
        "norm_ffn_g": f(inp["norm_ffn_g"]).reshape(nlw, 8, 128),
        "w_ff1": f(inp["w_ff1"]),
        "w_ff2": f(inp["w_ff2"]),
        "final_norm_g": f(inp["final_norm_g"]).reshape(8, 128),
        "ropec": _rope_consts(),
    }


def run_cores(x, c, positions, wmap, layers, final_norm, nlw, ncores):
    B, S, _ = x.shape
    nseq = B // ncores
    nc = _get_prog((S, nseq, tuple(layers), bool(final_norm), nlw))
    in_maps = []
    for i in range(ncores):
        m = dict(wmap)
        m["x"] = np.ascontiguousarray(x[i * nseq:(i + 1) * nseq])
        m["crows"] = np.ascontiguousarray(c[i * nseq:(i + 1) * nseq]).reshape(nseq * 8, 128)
        m["pos"] = np.ascontiguousarray(positions[i * nseq:(i + 1) * nseq]).astype(np.int32)
        in_maps.append(m)
    res = run_bass_kernel_spmd(nc, in_maps, core_ids=list(range(ncores)))
    return np.concatenate([np.asarray(r["out"]) for r in res.results], axis=0)


def kernel(**inputs):
    x = np.asarray(inputs["x"], dtype=np.float32)
    c = np.asarray(inputs["c"], dtype=np.float32)
    positions = np.asarray(inputs["positions"]).astype(np.int32)
    nlw = int(np.asarray(inputs["w_ada"]).shape[0])
    wmap = _weights_map(inputs, nlw)
    out = run_cores(x, c, positions, wmap, list(range(nlw)), True, nlw, NCORES)
    return out.astype(np.float32)
```

```python
import numpy as np
import concourse.bass as bass
import concourse.mybir as mybir
from concourse.bass_utils import run_bass_kernel_spmd

F32 = mybir.dt.float32
BF16 = mybir.dt.bfloat16
I32 = mybir.dt.int32
AF = mybir.ActivationFunctionType
ALU = mybir.AluOpType
AX = mybir.AxisListType

D = 1024
NCH = 8
DFF = 4096
EPS = 1e-6
NCORES = 8


class Buf:
    __slots__ = ("name", "lw", "rd", "psum")

    def __init__(self, name, psum=False, rd=None):
        self.name = name
        self.lw = None
        self.rd = list(rd) if rd else []
        self.psum = psum


class Chan:
    def __init__(self, sem):
        self.sem = sem
        self.cnt = 0


class Eng:
    def __init__(self, fw, name, is_pe=False):
        self.fw = fw
        self.name = name
        self.ops = []
        self.count = 0
        self.sem = None
        self.seen = {}
        self.is_pe = is_pe
        self.pend_r = []
        self.pend_w = []

    def _collect(self, reads, writes, is_dma=False, chan=None):
        waits = {}

        def need(m):
            k, v, en = m
            if v > waits.get(k, 0):
                waits[k] = v

        for b in reads:
            m = b.lw
            if m is not None:
                if m[2] == self.name and not is_dma:
                    if not self.is_pe:
                        need(m)
                else:
                    need(m)
            if b.psum:
                for r in b.rd:
                    if r[2] != self.name or is_dma:
                        need(r)
        for b in writes:
            m = b.lw
            if m is not None:
                if is_dma and chan is not None and m[0] == id(chan.sem):
                    pass
                elif m[2] == self.name and not is_dma:
                    pass
                else:
                    need(m)
            for r in b.rd:
                if r[2] != self.name or is_dma:
                    need(r)
        out = []
        for k, v in waits.items():
            if self.seen.get(k, 0) >= v:
                continue
            self.seen[k] = v
            out.append((self.fw.semobj[k], v))
        return out

    def op(self, fn, reads=(), writes=(), inc=True):
        waits = self._collect(reads, writes)
        self.pend_r.extend(reads)
        self.pend_w.extend(writes)
        if inc:
            self.count += 1
            mark = (id(self.sem), self.count, self.name)
            for b in self.pend_r:
                b.rd.append(mark)
            for b in self.pend_w:
                b.lw = mark
                b.rd = []
            self.pend_r = []
            self.pend_w = []
            self.ops.append((waits, fn, [(self.sem, 1)]))
        else:
            self.ops.append((waits, fn, []))

    def dma(self, fn, chan, reads=(), writes=()):
        waits = self._collect(reads, writes, is_dma=True, chan=chan)
        chan.cnt += 16
        mark = (id(chan.sem), chan.cnt, "dma")
        for b in reads:
            b.rd.append(mark)
        for b in writes:
            b.lw = mark
            b.rd = []
        self.ops.append((waits, fn, [(chan.sem, 16)]))

    def wait_for(self, bufs):
        waits = self._collect(bufs, ())
        self.ops.append((waits, None, []))


class FW:
    def __init__(self, nc):
        self.nc = nc
        self.semobj = {}
        self.chans = {}
        self.pe = Eng(self, "pe", is_pe=True)
        self.act = Eng(self, "act")
        self.dve = Eng(self, "dve")
        self.pool = Eng(self, "pool")
        self.sp = Eng(self, "sp")
        self.engs = [self.pe, self.act, self.dve, self.pool, self.sp]
        for e in self.engs:
            e.sem = self.new_sem("t_" + e.name)
        self.cur_barrier = []

    def new_sem(self, name):
        s = self.nc.alloc_semaphore(name)
        self.semobj[id(s)] = s
        return s

    def chan(self, name):
        if name not in self.chans:
            self.chans[name] = Chan(self.new_sem("d_" + name))
        return self.chans[name]

    def buf(self, name, psum=False, fresh=False):
        return Buf(name, psum=psum, rd=None if fresh else self.cur_barrier)

    def barrier(self):
        m = []
        for e in self.engs:
            if e.count > 0:
                m.append((id(e.sem), e.count, "barrier"))
        for c in self.chans.values():
            if c.cnt > 0:
                m.append((id(c.sem), c.cnt, "barrier"))
        self.cur_barrier = m

    @staticmethod
    def alias_into(dst_bufs, src_bufs):
        ms = []
        for b in src_bufs:
            if b.lw is not None:
                ms.append((b.lw[0], b.lw[1], "alias"))
            for r in b.rd:
                ms.append((r[0], r[1], "alias"))
        best = {}
        for k, v, _ in ms:
            if v > best.get(k, 0):
                best[k] = v
        ms = [(k, v, "alias") for k, v in best.items()]
        for d in dst_bufs:
            d.rd.extend(ms)

    def replay(self):
        nc = self.nc
        with nc.Block() as block:
            def run(rec):
                def body(e):
                    for waits, fn, incs in rec.ops:
                        for s, v in waits:
                            e.wait_ge(s, v)
                        if fn is None:
                            continue
                        ins = fn(e)
                        for s, v in incs:
                            ins.then_inc(s, v)
                return body
            block.tensor(run(self.pe))
            block.scalar(run(self.act))
            block.vector(run(self.dve))
            block.gpsimd(run(self.pool))
            block.sync(run(self.sp))


def MM(fw, out, lhsT, rhs, start, stop, r, w, inc):
    fw.pe.op(lambda e: e.matmul(out, lhsT=lhsT, rhs=rhs, start=start, stop=stop), r, w, inc)


def TR(fw, out, in_, ident, r, w, inc):
    fw.pe.op(lambda e: e.transpose(out, in_, ident), r, w, inc)


def ACTV(fw, out, in_, func, r, w, bias=None, scale=None, accum_out=None):
    kw = {}
    if bias is not None:
        kw["bias"] = bias
    if scale is not None:
        kw["scale"] = scale
    if accum_out is not None:
        kw["accum_out"] = accum_out
    fw.act.op(lambda e: e.activation(out=out, in_=in_, func=func, **kw), r, w)


def TT(eng, out, in0, in1, op, r, w):
    eng.op(lambda e: e.tensor_tensor(out=out, in0=in0, in1=in1, op=op), r, w)


def TS(eng, out, in0, s1, s2, op0, op1, r, w):
    if s2 is None:
        eng.op(lambda e: e.tensor_scalar(out=out, in0=in0, scalar1=s1, scalar2=None, op0=op0), r, w)
    else:
        eng.op(lambda e: e.tensor_scalar(out=out, in0=in0, scalar1=s1, scalar2=s2, op0=op0, op1=op1), r, w)


def STT(fw, out, in0, scalar, in1, op0, op1, r, w):
    fw.dve.op(lambda e: e.scalar_tensor_tensor(out=out, in0=in0, scalar=scalar, in1=in1, op0=op0, op1=op1), r, w)


def CP(eng, out, in_, r, w):
    eng.op(lambda e: e.tensor_copy(out=out, in_=in_), r, w)


def MS(eng, ap, val, r, w):
    eng.op(lambda e: e.memset(ap, val), r, w)


def RED(fw, out, in_, r, w):
    fw.dve.op(lambda e: e.tensor_reduce(out=out, in_=in_, axis=AX.X, op=ALU.add), r, w)


def RECIP(fw, out, in_, r, w):
    fw.dve.op(lambda e: e.reciprocal(out=out, in_=in_), r, w)


def DMA(eng, chan, out, in_, r, w):
    eng.dma(lambda e: e.dma_start(out=out, in_=in_), chan, r, w)


def build_program(S, NSEQ, LAYERS, FINAL_NORM, NLW):
    assert S % 512 == 0
    NG = S // 512
    NT = S // 128
    KB = NT
    nc = bass.Bass("TRN2", target_bir_lowering=False)
    fw = FW(nc)
    pe, act, dve, pool, sp = fw.pe, fw.act, fw.dve, fw.pool, fw.sp

    def din(name, shape, dt=F32):
        return nc.dram_tensor(name, shape, dt, kind="ExternalInput").ap()

    x_d = din("x", [NSEQ, S, D])
    crow_d = din("crows", [NSEQ * 8, 128])
    pos_d = din("pos", [NSEQ, S], I32)
    rc_d = din("ropec", [64, 4])
    w_ada = din("w_ada", [NLW, D, 6 * D])
    b_ada = din("b_ada", [NLW, 48, 128])
    g_mix = din("norm_mix_g", [NLW, 8, 128])
    w_in = din("w_in", [NLW, D, 1440])
    ws_d = din("gmlp_ws", [NLW, 8, 128, 128])
    bs_d = din("gmlp_bs", [NLW, 8, 128])
    gq_d = din("mla_q_norm_g", [NLW, 2, 128])
    gkv_d = din("mla_kv_norm_g", [NLW, 1, 128])
    w_uq = din("mla_w_uq", [NLW, 256, 768])
    w_ukv = din("mla_w_ukv", [NLW, 128, 1024])
    ggm_d = din("out_norm_gmlp_g", [NLW, 4, 128])
    gam_d = din("out_norm_mla_g", [NLW, 4, 128])
    w_out = din("w_out", [NLW, D, D])
    g_ffn = din("norm_ffn_g", [NLW, 8, 128])
    w_ff1 = din("w_ff1", [NLW, D, DFF])
    w_ff2 = din("w_ff2", [NLW, DFF, D])
    gf_d = din("final_norm_g", [8, 128])
    out_d = nc.dram_tensor("out", [NSEQ, S, D], F32, kind="ExternalOutput").ap()

    P_XT = 0
    P_HT = P_XT + 8 * S * 4
    P_TAB = P_HT + 8 * S * 2
    P_CONST = P_TAB + S * 4
    CONST_SZ = 4096
    V0 = P_CONST + CONST_SZ
    TOTAL = 212736
    STATIC_SZ = 7680 + 2048 + 4096 + 8192 + 2048
    ST0 = TOTAL - STATIC_SZ
    DA_SZ = ST0 - V0
    arena = nc.alloc_sbuf_tensor("arena", [128, TOTAL // 2], BF16)

    def ar(off, shape, dt):
        n = int(np.prod(shape))
        sz = n * (2 if dt == BF16 else 4)
        assert off % 4 == 0 and off + sz <= TOTAL, (off, sz)
        ap = arena[:, off // 2:(off + sz) // 2]
        if dt != BF16:
            ap = ap.bitcast(dt)
        if len(shape) == 2:
            ap = ap.rearrange("p (a b) -> p a b", a=shape[0])
        elif len(shape) == 3:
            ap = ap.rearrange("p (a b c) -> p a b c", a=shape[0], b=shape[1])
        return ap

    class DAlloc:
        def __init__(self, base, limit):
            self.base = base
            self.limit = limit
            self.cur = base

        def reset(self):
            self.cur = self.base

        def get(self, shape, dt):
            n = int(np.prod(shape)) * (2 if dt == BF16 else 4)
            n = (n + 31) // 32 * 32
            off = self.cur
            self.cur += n
            assert self.cur <= self.limit, ("DA overflow", self.cur - self.base, self.limit - self.base)
            return ar(off, shape, dt)

    da = DAlloc(V0, ST0)

    xT = ar(P_XT, [8, S], F32)
    hT = ar(P_HT, [8, S], BF16)
    tab = ar(P_TAB, [S], F32)
    c0 = P_CONST
    ident_f = ar(c0, [128], F32); c0 += 512
    ident_b = ar(c0, [128], BF16); c0 += 256
    mask_b = ar(c0, [128], BF16); c0 += 256
    ones_b = ar(c0, [128], BF16); c0 += 256
    NL = len(LAYERS)
    modc = ar(c0, [NL * NSEQ, 48], F32); c0 += NL * NSEQ * 48 * 4
    colsL = ar(c0, [NL, 96], F32); c0 += NL * 96 * 4
    colsM = ar(c0, [32], F32); c0 += 128
    cact = ar(c0, [NSEQ * 8], BF16); c0 += 64
    rcs = ar(c0, [4], F32); c0 += 16
    epsc = ar(c0, [1], F32); c0 += 16
    smalls = ar(c0, [64], F32); c0 += 256
    smalls2 = ar(c0, [64], F32); c0 += 256
    smalls3 = ar(c0, [64], F32); c0 += 256
    assert c0 <= P_CONST + CONST_SZ, c0 - P_CONST

    s0 = ST0
    w_inB = ar(s0, [8, 480], BF16); s0 += 7680
    w_ukvS = ar(s0, [1024], BF16); s0 += 2048
    w_uqS = ar(s0, [2, 8, 128], BF16); s0 += 4096
    w_outA = ar(s0, [4, 1024], BF16); s0 += 8192
    wsT = ar(s0, [8, 128], BF16); s0 += 2048

    B_winB = fw.buf("w_inB", fresh=True)
    B_wukv = fw.buf("w_ukv", fresh=True)
    B_wuq = fw.buf("w_uq", fresh=True)
    B_woutA = fw.buf("w_outA", fresh=True)
    B_wsT = fw.buf("wsT", fresh=True)

    pb = [nc.alloc_psum_tensor(f"pb{i}", [128, 512], F32) for i in range(8)]
    PB = [fw.buf(f"pb{i}", psum=True, fresh=True) for i in range(8)]

    B_xT = [[fw.buf(f"xT{m}_{g}", fresh=True) for g in range(NG)] for m in range(8)]
    B_hT = [[fw.buf(f"hT{k}_{g}", fresh=True) for g in range(NG)] for k in range(8)]
    B_tab = fw.buf("tab", fresh=True)
    B_const = fw.buf("const", fresh=True)
    B_modc = fw.buf("modc", fresh=True)
    B_cols = fw.buf("cols", fresh=True)
    B_out = fw.buf("out", fresh=True)
    ch_out = [fw.chan("out0"), fw.chan("out1")]

    def gc(g):
        return slice(g * 512, (g + 1) * 512)

    MS(pool, ident_f, 1.0, [], [B_const])
    pool.op(lambda e: e.affine_select(out=ident_f, in_=ident_f, pattern=[[-1, 128]], compare_op=ALU.is_equal,
                                      fill=0.0, base=0, channel_multiplier=1), [B_const], [B_const])
    CP(pool, ident_b, ident_f, [B_const], [B_const])
    MS(pool, ones_b, 1.0, [], [B_const])
    da.reset()
    mask_f = da.get([128], F32)
    B_mf = fw.buf("mask_f")
    MS(pool, mask_f, 1.0, [], [B_mf])
    pool.op(lambda e: e.affine_select(out=mask_f, in_=mask_f, pattern=[[1, 128]], compare_op=ALU.is_ge,
                                      fill=0.0, base=0, channel_multiplier=-1), [B_mf], [B_mf])
    CP(pool, mask_b, mask_f, [B_mf], [B_const])
    MS(pool, epsc, EPS, [], [B_const])
    DMA(sp, fw.chan("rc"), rcs[0:64, :], rc_d, [], [B_const])

    rows = [da.get([128], F32) for _ in range(NL + 1)]
    B_rows = [fw.buf(f"rows{i}") for i in range(NL + 1)]
    for i in range(NL + 1):
        MS(pool, rows[i], 0.0, [], [B_rows[i]])
    ch_rows_l = [fw.chan(f"rows{i}") for i in range(NL + 1)]
    for li, l in enumerate(LAYERS):
        rt = rows[li]
        for (r0, n, src) in [(0, 8, g_mix[l]), (8, 8, g_ffn[l]), (16, 4, ggm_d[l]), (20, 4, gam_d[l]),
                             (24, 2, gq_d[l]), (26, 1, gkv_d[l]), (27, 8, bs_d[l]), (35, 48, b_ada[l])]:
            DMA(sp, ch_rows_l[li], rt[r0:r0 + n, :], src, [], [B_rows[li]])
    DMA(sp, ch_rows_l[NL], rows[NL][0:NSEQ * 8, :], crow_d, [], [B_rows[NL]])
    DMA(sp, ch_rows_l[NL], rows[NL][16:24, :], gf_d, [], [B_rows[NL]])
    for i in range(NL + 1):
        TR(fw, pb[0][:, 0:128], rows[i], ident_f, [B_rows[i], B_const], [PB[0]], True)
        if i < NL:
            CP(dve, colsL[:, i, 0:96], pb[0][:, 0:96], [PB[0]], [B_cols])
        else:
            CP(dve, colsM[:, 0:32], pb[0][:, 0:32], [PB[0]], [B_cols])
    ACTV(fw, cact[:, :], colsM[:, 0:NSEQ * 8], AF.Silu, [B_cols], [B_const])

    def emit_norm(A_cols, B_cols_, tag):
        sq = da.get([8, 512], BF16)
        xn = [da.get([512], F32) for _ in range(2)]
        rstd = da.get([512], F32)
        B_sq = fw.buf("sq" + tag); B_xn = [fw.buf("xn0" + tag), fw.buf("xn1" + tag)]; B_rstd = fw.buf("rstd" + tag)
        for g in range(NG):
            for fc in range(8):
                ACTV(fw, sq[:, fc, :], xT[:, fc, gc(g)], AF.Square, [B_xT[fc][g]], [B_sq])
            for fc in range(8):
                MM(fw, pb[7][:, :], ones_b, sq[:, fc, :], fc == 0, fc == 7, [B_sq, B_const], [PB[7]], fc == 7)
            ACTV(fw, rstd, pb[7][:, :], AF.Ln, [PB[7], B_const], [B_rstd], bias=epsc[:, 0:1], scale=1.0 / D)
            ACTV(fw, rstd, rstd, AF.Exp, [B_rstd], [B_rstd], scale=-0.5)
            for fc in range(8):
                j = fc % 2
                TT(dve, xn[j], xT[:, fc, gc(g)], rstd, ALU.mult, [B_xT[fc][g], B_rstd], [B_xn[j]])
                if fc % 2 == 0:
                    TS(pool, hT[:, fc, gc(g)], xn[j], A_cols[:, fc:fc + 1], B_cols_[:, fc:fc + 1], ALU.mult, ALU.add,
                       [B_xn[j], B_modc], [B_hT[fc][g]])
                else:
                    ACTV(fw, hT[:, fc, gc(g)], xn[j], AF.Identity, [B_xn[j], B_modc], [B_hT[fc][g]],
                         bias=B_cols_[:, fc:fc + 1], scale=A_cols[:, fc:fc + 1])

    def emit_mod():
        wada = [da.get([8, 768], BF16) for _ in range(2)]
        B_wada = [fw.buf(f"wada{i}") for i in range(2)]
        ch_wada = [fw.chan("wada0"), fw.chan("wada1")]
        modT = da.get([48, NSEQ], F32)
        B_modT = fw.buf("modT")
        cact3 = cact.rearrange("p (s k) -> p s k", s=NSEQ)
        it = 0
        for li, l in enumerate(LAYERS):
            for cb in range(8):
                sl = it % 2
                it += 1
                src = w_ada[l][:, cb * 768:(cb + 1) * 768].rearrange("(k p) n -> p k n", p=128)
                DMA(pool, ch_wada[sl], wada[sl], src, [], [B_wada[sl]])
                for ch in range(6):
                    col0 = (cb * 6 + ch) * NSEQ
                    for k in range(8):
                        MM(fw, pb[1][:, col0:col0 + NSEQ], wada[sl][:, k, ch * 128:(ch + 1) * 128], cact3[:, :, k],
                           k == 0, k == 7, [B_wada[sl], B_const], [PB[1]], (k == 7 and ch == 5))
            TT(dve, modT, pb[1][:, 0:48 * NSEQ].rearrange("p (a b) -> p a b", b=NSEQ),
               colsL[:, li, 35:83].unsqueeze(2).broadcast_to([128, 48, NSEQ]), ALU.add, [PB[1], B_cols], [B_modT])
            for s in range(NSEQ):
                mc = modc[:, li * NSEQ + s, :]
                STT(fw, mc[:, 0:8], modT[:, 8:16, s], 1.0, colsL[:, li, 0:8], ALU.add, ALU.mult, [B_modT, B_cols], [B_modc])
                CP(dve, mc[:, 8:16], modT[:, 0:8, s], [B_modT], [B_modc])
                CP(dve, mc[:, 16:24], modT[:, 16:24, s], [B_modT], [B_modc])
                STT(fw, mc[:, 24:32], modT[:, 32:40, s], 1.0, colsL[:, li, 8:16], ALU.add, ALU.mult, [B_modT, B_cols], [B_modc])
                CP(dve, mc[:, 32:40], modT[:, 24:32, s], [B_modT], [B_modc])
                CP(dve, mc[:, 40:48], modT[:, 40:48, s], [B_modT], [B_modc])

    for s in range(NSEQ):
        if s > 0:
            fw.barrier()
            da.reset()
        xin = [da.get([1024], F32) for _ in range(2)]
        B_xin = [fw.buf("xin0"), fw.buf("xin1")]
        ch_xin = [fw.chan("xin0"), fw.chan("xin1")]
        posi = da.get([S], I32); posf = da.get([S], F32); kf = da.get([S], F32); ki = da.get([S], I32)
        B_pos = fw.buf("pos")
        for t in range(NT):
            j = t % 2
            DMA(sp, ch_xin[j], xin[j], x_d[s, t * 128:(t + 1) * 128, :], [], [B_xin[j]])
            for half in range(2):
                bank = (2 * t + half) % 4
                for q in range(4):
                    fc = half * 4 + q
                    TR(fw, pb[bank][:, q * 128:(q + 1) * 128], xin[j][:, fc * 128:(fc + 1) * 128], ident_f,
                       [B_xin[j], B_const], [PB[bank]], q == 3)
                eng = dve if half == 0 else None
                dst = xT[:, half * 4:half * 4 + 4, t * 128:(t + 1) * 128]
                src = pb[bank][:, :].rearrange("p (a b) -> p a b", a=4)
                wb = [B_xT[half * 4 + q][t // 4] for q in range(4)]
                if half == 0:
                    CP(dve, dst, src, [PB[bank]], wb)
                else:
                    ACTV(fw, dst, src, AF.Copy, [PB[bank]], wb)
        DMA(sp, fw.chan("pos"), posi[0:64, :], pos_d[s:s + 1, :].partition_broadcast(64), [], [B_pos])
        C1 = 6.28125
        C2 = 2 * np.pi - 6.28125
        P64 = slice(0, 64)
        CP(dve, posf[P64, :], posi[P64, :], [B_pos], [B_pos])
        TS(dve, posf[P64, :], posf[P64, :], rcs[P64, 0:1], None, ALU.mult, None, [B_pos, B_const], [B_pos])
        TS(dve, kf[P64, :], posf[P64, :], 1.0 / (2 * np.pi), None, ALU.mult, None, [B_pos], [B_pos])
        CP(dve, ki[P64, :], kf[P64, :], [B_pos], [B_pos])
        CP(dve, kf[P64, :], ki[P64, :], [B_pos], [B_pos])
        STT(fw, posf[P64, :], kf[P64, :], -C1, posf[P64, :], ALU.mult, ALU.add, [B_pos], [B_pos])
        STT(fw, posf[P64, :], kf[P64, :], -C2, posf[P64, :], ALU.mult, ALU.add, [B_pos], [B_pos])
        TS(dve, posf[P64, :], posf[P64, :], rcs[P64, 1:2], None, ALU.add, None, [B_pos, B_const], [B_pos])
        TS(dve, kf[P64, :], posf[P64, :], float(np.pi), float(-2 * np.pi), ALU.is_gt, ALU.mult, [B_pos], [B_pos])
        TT(dve, posf[P64, :], posf[P64, :], kf[P64, :], ALU.add, [B_pos], [B_pos])
        ACTV(fw, tab[P64, :], posf[P64, :], AF.Sin, [B_pos, B_const], [B_tab], scale=rcs[P64, 2:3])

        if s == 0:
            emit_mod()

        for li, l in enumerate(LAYERS):
            mc = modc[:, li * NSEQ + s, :]
            A1, B1, G1 = mc[:, 0:8], mc[:, 8:16], mc[:, 16:24]
            A2, B2, G2 = mc[:, 24:32], mc[:, 32:40], mc[:, 40:48]
            cl = colsL[:, li, :]
            ggm, gam, gq, gkv, bsc = cl[:, 16:20], cl[:, 20:24], cl[:, 24:26], cl[:, 26:27], cl[:, 27:35]

            DMA(pool, fw.chan("w_inB"), w_inB[:, :, 0:416],
                w_in[l][:, 1024:1440].rearrange("(k p) n -> p k n", p=128), [], [B_winB])
            DMA(pool, fw.chan("w_inB"), w_inB[:, :, 416:432],
                w_in[l][:, 1424:1440].rearrange("(k p) n -> p k n", p=128), [], [B_winB])
            DMA(pool, fw.chan("w_inB"), w_inB[:, :, 432:448],
                w_in[l][:, 1408:1424].rearrange("(k p) n -> p k n", p=128), [], [B_winB])
            ukv_src = w_ukv[l].rearrange("p (h two d) -> p two h d", h=8, two=2)
            ukv_dst = w_ukvS.rearrange("p (two h d) -> p two h d", two=2, h=8)
            for two in range(2):
                DMA(pool, fw.chan("w_ukv"), ukv_dst[:, two, :, :], ukv_src[:, two, :, :], [], [B_wukv])
            uq_src = w_uq[l].rearrange("(c p) (h d) -> p c h d", p=128, d=96)
            for (d0, d1, s0_, s1_) in [(0, 32, 64, 96), (32, 48, 80, 96), (48, 64, 64, 80), (64, 128, 0, 64)]:
                for c in range(2):
                    DMA(pool, fw.chan("w_uq"), w_uqS[:, c, :, d0:d1], uq_src[:, c, :, s0_:s1_], [], [B_wuq])
            DMA(pool, fw.chan("w_outA"), w_outA, w_out[l][512:1024, :].rearrange("(c p) n -> p c n", p=128),
                [], [B_woutA])

            fw.barrier()
            da.reset()
            w_inA = da.get([8, 1024], BF16); B_winA = fw.buf("w_inA")
            w_outG = da.get([4, 1024], BF16); B_woutG = fw.buf("w_outG")
            ws_st = da.get([8, 128], F32); B_wsst = fw.buf("ws_st")
            DMA(pool, fw.chan("w_inA"), w_inA, w_in[l][:, 0:1024].rearrange("(k p) n -> p k n", p=128), [], [B_winA])
            DMA(pool, fw.chan("w_outG"), w_outG, w_out[l][0:512, :].rearrange("(c p) n -> p c n", p=128), [], [B_woutG])
            DMA(sp, fw.chan("ws_st"), ws_st, ws_d[l].rearrange("g t s -> t g s"), [], [B_wsst])
            for g8 in range(8):
                bank = g8 // 4
                TR(fw, pb[bank][:, (g8 % 4) * 128:(g8 % 4 + 1) * 128], ws_st[:, g8, :], ident_f,
                   [B_wsst, B_const], [PB[bank]], g8 % 4 == 3)
            for bank in range(2):
                TT(dve, wsT[:, bank * 4:bank * 4 + 4, :], pb[bank][:, :].rearrange("p (a b) -> p a b", a=4),
                   mask_b.unsqueeze(1).broadcast_to([128, 4, 128]), ALU.mult, [PB[bank], B_const], [B_wsT])

            emit_norm(A1, B1, "n1")

            yTg = [da.get([4, 512], BF16) for _ in range(2)]
            B_yTg = [fw.buf("yTg0"), fw.buf("yTg1")]
            R3 = 3
            gu = [da.get([512], BF16) for _ in range(R3)]; B_gu = [fw.buf(f"gu{i}") for i in range(R3)]
            gv = [da.get([512], F32) for _ in range(R3)]; B_gv = [fw.buf(f"gv{i}") for i in range(R3)]
            sqt = [da.get([512], F32) for _ in range(R3)]; B_sqt = [fw.buf(f"sqt{i}") for i in range(R3)]
            vn = [da.get([512], BF16) for _ in range(R3)]; B_vn = [fw.buf(f"vn{i}") for i in range(R3)]
            ygn = [da.get([512], BF16) for _ in range(R3)]; B_ygn = [fw.buf(f"ygn{i}") for i in range(R3)]
            st = [smalls, smalls2, smalls3]; B_st = [fw.buf(f"st{i}") for i in range(R3)]

            def g_s0(t):
                j = t % 2
                g = t // 4
                tc_ = slice(t * 128, (t + 1) * 128)
                for k in range(8):
                    MM(fw, pb[0 + j][:, :], hT[:, k, tc_], w_inA[:, k, 0:512], k == 0, k == 7,
                       [B_hT[k][g], B_winA], [PB[0 + j]], k == 7)
                    if k % 2 == 1:
                        yield
                for k in range(8):
                    MM(fw, pb[2 + j][:, :], hT[:, k, tc_], w_inA[:, k, 512:1024], k == 0, k == 7,
                       [B_hT[k][g], B_winA], [PB[2 + j]], k == 7)
                    if k % 2 == 1:
                        yield

            def g_s1(t):
                j = t % 2
                r = t % R3
                s_ = st[r]
                ACTV(fw, gu[r], pb[0 + j][:, :], AF.Gelu_apprx_tanh, [PB[0 + j]], [B_gu[r]])
                yield
                ACTV(fw, gv[r], pb[2 + j][:, :], AF.Gelu_apprx_tanh, [PB[2 + j]], [B_gv[r]])
                yield
                gv3 = gv[r].rearrange("p (g d) -> p g d", g=8)
                sq3 = sqt[r].rearrange("p (g d) -> p g d", g=8)
                TT(pool, sqt[r], gv[r], gv[r], ALU.mult, [B_gv[r]], [B_sqt[r]])
                yield
                RED(fw, s_[:, 0:8], gv3, [B_gv[r]], [B_st[r]])
                yield
                RED(fw, s_[:, 8:16], sq3, [B_sqt[r]], [B_st[r]])
                yield
                TS(dve, s_[:, 0:8], s_[:, 0:8], 1.0 / 64, None, ALU.mult, None, [B_st[r]], [B_st[r]])
                yield
                TT(dve, s_[:, 16:24], s_[:, 0:8], s_[:, 0:8], ALU.mult, [B_st[r]], [B_st[r]])
                yield
                STT(fw, s_[:, 8:16], s_[:, 8:16], 1.0 / 64, s_[:, 16:24], ALU.mult, ALU.subtract, [B_st[r]], [B_st[r]])
                yield
                ACTV(fw, s_[:, 8:16], s_[:, 8:16], AF.Ln, [B_st[r], B_const], [B_st[r]], bias=epsc[:, 0:1])
                yield
                ACTV(fw, s_[:, 8:16], s_[:, 8:16], AF.Exp, [B_st[r]], [B_st[r]], scale=-0.5)
                yield

            def g_s2(t):
                j = t % 2
                r = t % R3
                s_ = st[r]
                gv3 = gv[r].rearrange("p (g d) -> p g d", g=8)
                sq3 = sqt[r].rearrange("p (g d) -> p g d", g=8)
                TT(dve, sq3, gv3, s_[:, 0:8].unsqueeze(2).broadcast_to([128, 8, 64]), ALU.subtract,
                   [B_gv[r], B_st[r]], [B_sqt[r]])
                yield
                TT(pool, vn[r].rearrange("p (g d) -> p g d", g=8), sq3,
                   s_[:, 8:16].unsqueeze(2).broadcast_to([128, 8, 64]), ALU.mult, [B_sqt[r], B_st[r]], [B_vn[r]])
                yield
                for g8 in range(8):
                    MM(fw, pb[4 + j][:, g8 * 64:(g8 + 1) * 64], wsT[:, g8, :], vn[r][:, g8 * 64:(g8 + 1) * 64],
                       True, True, [B_wsT, B_vn[r]], [PB[4 + j]], g8 == 7)
                    if g8 % 2 == 1:
                        yield
                TT(dve, sq3, pb[4 + j][:, :].rearrange("p (g d) -> p g d", g=8),
                   bsc.unsqueeze(2).broadcast_to([128, 8, 64]), ALU.add, [PB[4 + j], B_cols], [B_sqt[r]])
                yield
                TT(pool, gv[r], sqt[r], gu[r], ALU.mult, [B_sqt[r], B_gu[r]], [B_gv[r]])
                yield
                MS(dve, s_[:, 24:25], 0.0, [], [B_st[r]])
                yield
                ACTV(fw, sqt[r], gv[r], AF.Square, [B_gv[r], B_st[r]], [B_sqt[r], B_st[r]], accum_out=s_[:, 24:25])
                yield
                ACTV(fw, s_[:, 24:25], s_[:, 24:25], AF.Ln, [B_st[r], B_const], [B_st[r]], bias=epsc[:, 0:1], scale=1.0 / 512)
                yield
                ACTV(fw, s_[:, 24:25], s_[:, 24:25], AF.Exp, [B_st[r]], [B_st[r]], scale=-0.5)
                yield

            def g_s3(t):
                r = t % R3
                g = t // 4
                s_ = st[r]
                TS(dve, ygn[r], gv[r], s_[:, 24:25], None, ALU.mult, None, [B_gv[r], B_st[r]], [B_ygn[r]])
                yield
                ptr = pb[6][:, :].bitcast(BF16)
                for c in range(4):
                    TR(fw, ptr[:, c * 128:(c + 1) * 128], ygn[r][:, c * 128:(c + 1) * 128], ident_b,
                       [B_ygn[r], B_const], [PB[6]], c == 3)
                    if c % 2 == 1:
                        yield
                TT(dve, yTg[g % 2][:, :, (t % 4) * 128:(t % 4 + 1) * 128],
                   ptr[:, 0:512].rearrange("p (a b) -> p a b", a=4),
                   ggm.unsqueeze(2).broadcast_to([128, 4, 128]), ALU.mult, [PB[6], B_cols], [B_yTg[g % 2]])
                yield

            def g_s4(g):
                for m in range(8):
                    for c in range(4):
                        MM(fw, pb[7][:, :], w_outG[:, c, m * 128:(m + 1) * 128], yTg[g % 2][:, c, :],
                           c == 0, c == 3, [B_woutG, B_yTg[g % 2]], [PB[7]], c == 3)
                    yield
                    STT(fw, xT[:, m, gc(g)], pb[7][:, :], G1[:, m:m + 1], xT[:, m, gc(g)], ALU.mult, ALU.add,
                        [PB[7], B_modc, B_xT[m][g]], [B_xT[m][g]])
                    yield

            def round_robin(gens):
                gens = [g_ for g_ in gens if g_ is not None]
                while gens:
                    nxt = []
                    for g_ in gens:
                        try:
                            next(g_)
                            nxt.append(g_)
                        except StopIteration:
                            pass
                    gens = nxt

            for step in range(NT + 5):
                gl = []
                if step < NT:
                    gl.append(g_s0(step))
                if 0 <= step - 1 < NT:
                    gl.append(g_s1(step - 1))
                if 0 <= step - 2 < NT:
                    gl.append(g_s2(step - 2))
                if 0 <= step - 3 < NT:
                    gl.append(g_s3(step - 3))
                t4 = step - 4
                if 0 <= t4 < NT and t4 % 4 == 3:
                    gl.append(g_s4(t4 // 4))
                round_robin(gl)

            fw.barrier()
            da.reset()
            kT = da.get([8, S], BF16)
            vaug = da.get([KB, 4, 192], BF16)
            cqT = da.get([2, S], BF16)
            sqm = da.get([3, 512], BF16)
            rst = da.get([512], F32)
            ckv = [da.get([512], BF16) for _ in range(2)]
            t1 = da.get([512], F32)
            t2 = da.get([512], F32)
            B_kT = [[fw.buf(f"kT{h}_{g}") for g in range(NG)] for h in range(8)]
            B_kz = fw.buf("kz")
            B_vaug = [fw.buf(f"vaug{j}") for j in range(KB)]
            B_cq = [fw.buf(f"cq{g}") for g in range(NG)]
            B_sqm = fw.buf("sqm"); B_rst = fw.buf("rst"); B_ckv = [fw.buf("ckv0"), fw.buf("ckv1")]
            B_t1 = fw.buf("t1"); B_t2 = fw.buf("t2")
            MS(pool, kT[32:64, :, :], 0.0, [], [B_kz])
            for j in range(KB):
                MS(pool, vaug[:, j, :, 64:128], 1.0, [], [B_vaug[j]])
            def m_a(g):
                cols = [(0, 0, 128), (1, 128, 256), (2, 256, 384)]
                for (bank, c0_, c1_) in cols:
                    for k in range(8):
                        MM(fw, pb[bank][:, :], w_inB[:, k, c0_:c1_], hT[:, k, gc(g)], k == 0, k == 7,
                           [B_winB, B_hT[k][g]], [PB[bank]], k == 7)
                        if k % 2 == 1:
                            yield
                for k in range(8):
                    MM(fw, pb[3][0:64, :], w_inB[:, k, 384:448], hT[:, k, gc(g)], k == 0, k == 7,
                       [B_winB, B_hT[k][g]], [PB[3]], k == 7)
                    if k % 2 == 1:
                        yield
                for i in range(3):
                    ACTV(fw, sqm[:, i, :], pb[i][:, :], AF.Square, [PB[i]], [B_sqm])
                    yield
                MM(fw, pb[4][:, :], ones_b, sqm[:, 0, :], True, False, [B_sqm, B_const], [PB[4]], False)
                MM(fw, pb[4][:, :], ones_b, sqm[:, 1, :], False, True, [B_sqm, B_const], [PB[4]], True)
                MM(fw, pb[5][:, :], ones_b, sqm[:, 2, :], True, True, [B_sqm, B_const], [PB[5]], True)
                yield
                TT(dve, t1[0:32, :], pb[3][0:32, :], tab[0:32, gc(g)], ALU.mult, [PB[3], B_tab], [B_t1])
                yield
                TT(dve, t2[0:32, :], pb[3][32:64, :], tab[32:64, gc(g)], ALU.mult, [PB[3], B_tab], [B_t2])
                yield
                TT(pool, kT[0:32, :, gc(g)], t1[0:32, :].unsqueeze(1).broadcast_to([32, 8, 512]),
                   t2[0:32, :].unsqueeze(1).broadcast_to([32, 8, 512]), ALU.add, [B_t1, B_t2],
                   [B_kT[h][g] for h in range(8)])
                yield
                ACTV(fw, rst, pb[4][:, :], AF.Ln, [PB[4], B_const], [B_rst], bias=epsc[:, 0:1], scale=1.0 / 256)
                yield
                ACTV(fw, rst, rst, AF.Exp, [B_rst], [B_rst], scale=-0.5)
                yield
                for c in range(2):
                    STT(fw, cqT[:, c, gc(g)], pb[c][:, :], gq[:, c:c + 1], rst, ALU.mult, ALU.mult,
                        [PB[c], B_cols, B_rst], [B_cq[g]])
                    yield
                ACTV(fw, rst, pb[5][:, :], AF.Ln, [PB[5], B_const], [B_rst], bias=epsc[:, 0:1], scale=1.0 / 128)
                yield
                ACTV(fw, rst, rst, AF.Exp, [B_rst], [B_rst], scale=-0.5)
                yield
                cj = g % 2
                STT(fw, ckv[cj], pb[2][:, :], gkv[:, 0:1], rst, ALU.mult, ALU.mult, [PB[2], B_cols, B_rst], [B_ckv[cj]])
                yield

            def m_b(g):
                cj = g % 2
                nb_ = 0
                for h in range(8):
                    bank = 6 + (nb_ % 2)
                    nb_ += 1
                    MM(fw, pb[bank][64:128, :], w_ukvS[:, h * 64:(h + 1) * 64], ckv[cj], True, True,
                       [B_wukv, B_ckv[cj]], [PB[bank]], True)
                    yield
                    if h % 2 == 0:
                        ACTV(fw, kT[64:128, h, gc(g)], pb[bank][64:128, :], AF.Copy, [PB[bank]], [B_kT[h][g]])
                    else:
                        CP(dve, kT[64:128, h, gc(g)], pb[bank][64:128, :], [PB[bank]], [B_kT[h][g]])
                    yield
                for tt_ in range(4):
                    t = g * 4 + tt_
                    bank = 6 + (nb_ % 2)
                    nb_ += 1
                    MM(fw, pb[bank][:, :], ckv[cj][:, tt_ * 128:(tt_ + 1) * 128], w_ukvS[:, 512:1024], True, True,
                       [B_ckv[cj], B_wukv], [PB[bank]], True)
                    yield
                    dst = vaug[:, t, :, :].rearrange("p a (b c) -> p a b c", b=3)[:, :, 0:3:2, :]
                    src = pb[bank][:, :].rearrange("p (a b c) -> p a b c", a=4, b=2)
                    if tt_ % 2 == 0:
                        ACTV(fw, dst, src, AF.Copy, [PB[bank]], [B_vaug[t]])
                    else:
                        CP(dve, dst, src, [PB[bank]], [B_vaug[t]])
                    yield

            for step in range(NG + 1):
                gl = []
                if step < NG:
                    gl.append(m_a(step))
                if step >= 1:
                    gl.append(m_b(step - 1))
                round_robin(gl)

            hoff = [P_HT]

            def hget(shape, dt):
                if S < 2048:
                    return da.get(shape, dt)
                n = int(np.prod(shape)) * (2 if dt == BF16 else 4)
                n = (n + 31) // 32 * 32
                off = hoff[0]
                hoff[0] += n
                assert hoff[0] <= P_HT + 8 * S * 2
                return ar(off, shape, dt)

            NQR = 3
            qTr = [hget([512], BF16) for _ in range(NQR)]
            o_n2 = [hget([4, 512], F32) for _ in range(2)]
            rr = [hget([512], F32) for _ in range(2)]
            pT = [hget([512], BF16) for _ in range(3)]
            yTa = hget([4, 512], BF16)
            rsty, B_rsty = rst, B_rst
            rt1, rt2, B_rt1, B_rt2 = t1, t2, B_t1, B_t2
            allh = [B_hT[k][g] for k in range(8) for g in range(NG)]
            scr = []

            def sbuf_(name):
                b = fw.buf(name, fresh=True)
                FW.alias_into([b], allh)
                scr.append(b)
                return b

            B_qT = [sbuf_(f"qT{i}") for i in range(NQR)]
            B_on2 = [[sbuf_(f"on{i}_{c}") for c in range(4)] for i in range(2)]
            B_rr = [sbuf_("rr0"), sbuf_("rr1")]
            B_pT = [sbuf_(f"pT{i}") for i in range(3)]
            B_yTa = sbuf_("yTa")
            for i in range(NQR):
                MS(pool, qTr[i][32:64, :], 0.0, [], [B_qT[i]])

            SCALE = float(96 ** -0.5)
            units = [(Q, h) for Q in range(NG) for h in range(8)]

            def emit_q(u):
                Q, h = units[u]
                slot = u % NQR
                bank = u % 2
                for c in range(2):
                    MM(fw, pb[bank][:, :], w_uqS[:, c, h, :], cqT[:, c, gc(Q)], c == 0, c == 1,
                       [B_wuq, B_cq[Q]], [PB[bank]], c == 1)
                CP(dve, qTr[slot][64:128, :], pb[bank][64:128, :], [PB[bank]], [B_qT[slot]])
                TT(dve, rt1[0:32, :], pb[bank][0:32, :], tab[0:32, gc(Q)], ALU.mult, [PB[bank], B_tab], [B_rt1])
                TT(dve, rt2[0:32, :], pb[bank][32:64, :], tab[32:64, gc(Q)], ALU.mult, [PB[bank], B_tab], [B_rt2])
                TT(pool, qTr[slot][0:32, :], rt1[0:32, :], rt2[0:32, :], ALU.add, [B_rt1, B_rt2], [B_qT[slot]])

            jobs = []
            for u, (Q, h) in enumerate(units):
                for j in range(4 * Q + 4):
                    jobs.append((u, j))

            def emit_qk(n):
                u, j = jobs[n]
                Q, h = units[u]
                qoff = max(0, (j - 4 * Q) * 128)
                sb = 2 + (n % 3)
                MM(fw, pb[sb][:, qoff:512], kT[:, h, j * 128:(j + 1) * 128], qTr[u % NQR][:, qoff:512], True, True,
                   [B_kT[h][j // 4], B_kz, B_qT[u % NQR]], [PB[sb]], True)
                ps_ = n % 3
                ACTV(fw, pT[ps_][:, qoff:512], pb[sb][:, qoff:512], AF.Exp, [PB[sb]], [B_pT[ps_]], scale=SCALE)
                if j >= 4 * Q:
                    TT(pool, pT[ps_][:, qoff:qoff + 128], pT[ps_][:, qoff:qoff + 128], mask_b, ALU.mult,
                       [B_pT[ps_], B_const], [B_pT[ps_]])

            def emit_pv(n):
                u, j = jobs[n]
                Q, h = units[u]
                qoff = max(0, (j - 4 * Q) * 128)
                p = h // 2
                ob = 5 + (u % 2)
                last = (j == 4 * Q + 3)
                stat = vaug[:, j, p, 0:128] if h % 2 == 0 else vaug[:, j, p, 64:192]
                MM(fw, pb[ob][:, qoff:512], stat, pT[n % 3][:, qoff:512], j == 0, last,
                   [B_vaug[j], B_pT[n % 3]], [PB[ob]], last)
                if last:
                    if h % 2 == 0:
                        orow, rrow = slice(0, 64), slice(64, 128)
                    else:
                        orow, rrow = slice(64, 128), slice(0, 64)
                    k2 = u % 2
                    RECIP(fw, rr[k2][orow, :], pb[ob][rrow, :], [PB[ob]], [B_rr[k2]])
                    TT(dve, o_n2[Q % 2][orow, p, :], pb[ob][orow, :], rr[k2][orow, :], ALU.mult,
                       [PB[ob], B_rr[k2]], [B_on2[Q % 2][p]])

            def g_fin(Q):
                o_n = o_n2[Q % 2]
                B_on = B_on2[Q % 2]
                for c in range(4):
                    TT(pool, yTa[:, c, :], o_n[:, c, :], o_n[:, c, :], ALU.mult, [B_on[c]], [B_yTa])
                    yield
                for c in range(4):
                    MM(fw, pb[7][:, :], ones_b, yTa[:, c, :], c == 0, c == 3, [B_yTa, B_const], [PB[7]], c == 3)
                yield
                ACTV(fw, rsty, pb[7][:, :], AF.Ln, [PB[7], B_const], [B_rsty], bias=epsc[:, 0:1], scale=1.0 / 512)
                yield
                ACTV(fw, rsty, rsty, AF.Exp, [B_rsty], [B_rsty], scale=-0.5)
                yield
                for c in range(4):
                    STT(fw, yTa[:, c, :], o_n[:, c, :], gam[:, c:c + 1], rsty, ALU.mult, ALU.mult,
                        [B_on[c], B_cols, B_rsty], [B_yTa])
                    yield
                for m in range(8):
                    for c in range(4):
                        MM(fw, pb[7][:, :], w_outA[:, c, m * 128:(m + 1) * 128], yTa[:, c, :], c == 0, c == 3,
                           [B_woutA, B_yTa], [PB[7]], c == 3)
                    yield
                    STT(fw, xT[:, m, gc(Q)], pb[7][:, :], G1[:, m:m + 1], xT[:, m, gc(Q)], ALU.mult, ALU.add,
                        [PB[7], B_modc, B_xT[m][Q]], [B_xT[m][Q]])
                    yield

            NJ = len(jobs)
            emit_q(0)
            first_job_of = {}
            last_job_of = {}
            for n, (u, j) in enumerate(jobs):
                first_job_of.setdefault(u, n)
                last_job_of[u] = n
            LOOK = 2
            fin = None
            for n in range(NJ + LOOK):
                if n < NJ:
                    u, j = jobs[n]
                    if n == first_job_of[u] and u + 1 < len(units):
                        emit_q(u + 1)
                    emit_qk(n)
                if n >= LOOK:
                    m_ = n - LOOK
                    emit_pv(m_)
                    u, j = jobs[m_]
                    Q, h = units[u]
                    if m_ == last_job_of[u] and h == 7:
                        if fin is not None:
                            for _ in fin:
                                pass
                        fin = g_fin(Q)
                if fin is not None and n % 2 == 1:
                    try:
                        next(fin)
                    except StopIteration:
                        fin = None
            if fin is not None:
                for _ in fin:
                    pass
            FW.alias_into(allh, scr)

            fw.barrier()
            da.reset()
            NCG = 8
            RING = 3
            w1r = [da.get([8, 512], BF16) for _ in range(RING)]
            w2r = [da.get([4, 1024], BF16) for _ in range(RING)]
            B_w1 = [fw.buf(f"w1_{i}") for i in range(RING)]
            B_w2 = [fw.buf(f"w2_{i}") for i in range(RING)]
            aT = [da.get([4, 512], BF16) for _ in range(2)]
            B_aT = [fw.buf("aT0"), fw.buf("aT1")]
            rl = [da.get([512], F32) for _ in range(3)]
            B_rl = [fw.buf(f"rl{i}") for i in range(3)]

            def ffn_load(cg):
                sl = cg % RING
                DMA(pool, fw.chan(f"w1_{sl}"), w1r[sl],
                    w_ff1[l][:, cg * 512:(cg + 1) * 512].rearrange("(k p) n -> p k n", p=128), [], [B_w1[sl]])
                DMA(pool, fw.chan(f"w2_{sl}"), w2r[sl],
                    w_ff2[l][cg * 512:(cg + 1) * 512, :].rearrange("(c p) n -> p c n", p=128), [], [B_w2[sl]])

            fjobs = [(cg, g) for cg in range(NCG) for g in range(NG)]
            rlc = [0]

            def ff1(n):
                cg, g = fjobs[n]
                sl = cg % RING
                a = n % 2
                for c in range(4):
                    bank = (n * 4 + c) % 4
                    for k in range(8):
                        MM(fw, pb[bank][:, :], w1r[sl][:, k, c * 128:(c + 1) * 128], hT[:, k, gc(g)], k == 0, k == 7,
                           [B_w1[sl], B_hT[k][g]], [PB[bank]], k == 7)
                    ri = rlc[0] % 3
                    rlc[0] += 1
                    ACTV(fw, rl[ri], pb[bank][:, :], AF.Relu, [PB[bank]], [B_rl[ri]])
                    TT(pool, aT[a][:, c, :], rl[ri], rl[ri], ALU.mult, [B_rl[ri]], [B_aT[a]])

            def ff2(n):
                cg, g = fjobs[n]
                sl = cg % RING
                a = n % 2
                for m in range(8):
                    bank = 4 + (m % 4)
                    for c in range(4):
                        MM(fw, pb[bank][:, :], w2r[sl][:, c, m * 128:(m + 1) * 128], aT[a][:, c, :], c == 0, c == 3,
                           [B_w2[sl], B_aT[a]], [PB[bank]], c == 3)
                    STT(fw, xT[:, m, gc(g)], pb[bank][:, :], G2[:, m:m + 1], xT[:, m, gc(g)], ALU.mult, ALU.add,
                        [PB[bank], B_modc, B_xT[m][g]], [B_xT[m][g]])

            ffn_load(0)
            ffn_load(1)
            emit_norm(A2, B2, "n2")
            for n in range(len(fjobs) + 1):
                if n < len(fjobs):
                    ff1(n)
                if n >= 1:
                    ff2(n - 1)
                if n < len(fjobs):
                    cg, g = fjobs[n]
                    if g == 0 and cg + 2 < NCG:
                        ffn_load(cg + 2)

        fw.barrier()
        da.reset()
        xo = [da.get([1024], F32) for _ in range(2)]
        B_xo = [fw.buf("xo0"), fw.buf("xo1")]
        if FINAL_NORM:
            sq = da.get([8, 512], BF16); B_sq = fw.buf("fsq")
            rstd = da.get([512], F32); B_rstd = fw.buf("frstd")
            yt = da.get([8, 512], F32); B_yt = fw.buf("fyt")
            gfc = colsM[:, 16:24]
        for g in range(NG):
            if FINAL_NORM:
                for fc in range(8):
                    ACTV(fw, sq[:, fc, :], xT[:, fc, gc(g)], AF.Square, [B_xT[fc][g]], [B_sq])
                for fc in range(8):
                    MM(fw, pb[7][:, :], ones_b, sq[:, fc, :], fc == 0, fc == 7, [B_sq, B_const], [PB[7]], fc == 7)
                ACTV(fw, rstd, pb[7][:, :], AF.Ln, [PB[7], B_const], [B_rstd], bias=epsc[:, 0:1], scale=1.0 / D)
                ACTV(fw, rstd, rstd, AF.Exp, [B_rstd], [B_rstd], scale=-0.5)
                for fc in range(8):
                    STT(fw, yt[:, fc, :], xT[:, fc, gc(g)], gfc[:, fc:fc + 1], rstd, ALU.mult, ALU.mult,
                        [B_xT[fc][g], B_cols, B_rstd], [B_yt])
            for tt_ in range(4):
                t = g * 4 + tt_
                j = t % 2
                for half in range(2):
                    bank = (2 * t + half) % 4
                    for q in range(4):
                        fc = half * 4 + q
                        if FINAL_NORM:
                            src = yt[:, fc, tt_ * 128:(tt_ + 1) * 128]
                            rb = [B_yt, B_const]
                        else:
                            src = xT[:, fc, t * 128:(t + 1) * 128]
                            rb = [B_xT[fc][g], B_const]
                        TR(fw, pb[bank][:, q * 128:(q + 1) * 128], src, ident_f, rb, [PB[bank]], q == 3)
                    if half == 0:
                        CP(dve, xo[j][:, 0:512], pb[bank][:, :], [PB[bank]], [B_xo[j]])
                    else:
                        ACTV(fw, xo[j][:, 512:1024], pb[bank][:, :], AF.Copy, [PB[bank]], [B_xo[j]])
                DMA(sp, ch_out[j], out_d[s, t * 128:(t + 1) * 128, :], xo[j], [B_xo[j]], [B_out])

    for c in ch_out:
        if c.cnt > 0:
            sem, cnt = c.sem, c.cnt
            sp.ops.append(([(sem, cnt)], None, []))
    fw.replay()
    return nc


def _rope_consts():
    rc = np.zeros((64, 4), np.float32)
    freqs = (10000.0 ** (-np.arange(0, 32, 2, dtype=np.float32) / np.float32(32))).astype(np.float32)
    for p in range(64):
        rc[p, 0] = freqs[p % 16]
    rc[0:32, 1] = np.pi / 2
    rc[0:32, 2] = 1.0
    rc[32:48, 2] = -1.0
    rc[48:64, 2] = 1.0
    return rc


_PROG_CACHE = {}


def _get_prog(key):
    if key not in _PROG_CACHE:
        _PROG_CACHE[key] = build_program(*key)
    return _PROG_CACHE[key]


def _weights_map(inp, nlw):
    f = lambda a: np.ascontiguousarray(np.asarray(a, dtype=np.float32))
    return {
        "w_ada": f(inp["w_ada"]),
        "b_ada": f(inp["b_ada"]).reshape(nlw, 48, 128),
        "norm_mix_g": f(inp["norm_mix_g"]).reshape(nlw, 8, 128),
        "w_in": f(inp["w_in"]),
        "gmlp_ws": f(inp["gmlp_ws"]),
        "gmlp_bs": f(inp["gmlp_bs"]),
        "mla_q_norm_g": f(inp["mla_q_norm_g"]).reshape(nlw, 2, 128),
        "mla_kv_norm_g": f(inp["mla_kv_norm_g"]).reshape(nlw, 1, 128),
        "mla_w_uq": f(inp["mla_w_uq"]),
        "mla_w_ukv": f(inp["mla_w_ukv"]),
        "out_norm_gmlp_g": f(inp["out_norm_gmlp_g"]).reshape(nlw, 4, 128),
        "out_norm_mla_g": f(inp["out_norm_mla_g"]).reshape(nlw, 4, 128),
        "w_out": f(inp["w_out"]),
        "norm_ffn_g": f(inp["norm_ffn_g"]).reshape(nlw, 8, 128),
        "w_ff1": f(inp["w_ff1"]),
        "w_ff2": f(inp["w_ff2"]),
        "final_norm_g": f(inp["final_norm_g"]).reshape(8, 128),
        "ropec": _rope_consts(),
    }


def run_cores(x, c, positions, wmap, layers, final_norm, nlw, ncores):
    B, S, _ = x.shape
    nseq = B // ncores
    nc = _get_prog((S, nseq, tuple(layers), bool(final_norm), nlw))
    in_maps = []
    for i in range(ncores):
        m = dict(wmap)
        m["x"] = np.ascontiguousarray(x[i * nseq:(i + 1) * nseq])
        m["crows"] = np.ascontiguousarray(c[i * nseq:(i + 1) * nseq]).reshape(nseq * 8, 128)
        m["pos"] = np.ascontiguousarray(positions[i * nseq:(i + 1) * nseq]).astype(np.int32)
        in_maps.append(m)
    res = run_bass_kernel_spmd(nc, in_maps, core_ids=list(range(ncores)))
    return np.concatenate([np.asarray(r["out"]) for r in res.results], axis=0)


def kernel(**inputs):
    x = np.asarray(inputs["x"], dtype=np.float32)
    c = np.asarray(inputs["c"], dtype=np.float32)
    positions = np.asarray(inputs["positions"]).astype(np.int32)
    nlw = int(np.asarray(inputs["w_ada"]).shape[0])
    wmap = _weights_map(inputs, nlw)
    out = run_cores(x, c, positions, wmap, list(range(nlw)), True, nlw, NCORES)
    return out.astype(np.float32)
```
